# Optimizing a Trainium2 kernel written in Bass

```python
import math
import jax
import jax.numpy as jnp
from jax import lax
import numpy as np

D_MODEL = 2048
BATCH = 4
SEQ = 4096
DEPTH = 4

D_MIX = D_MODEL
RET_HEADS = 4
RET_DK = D_MIX // 16
RET_DV = 2 * RET_DK
RET_CHUNK = 128
ROPE_BASE = 10000.0
S5_WIDTH = D_MIX // 4
S5_GROUP = 16
S5_GROUPS = S5_WIDTH // S5_GROUP
S5_STATE = 64
GLA_HEADS = 4
GLA_DV = (D_MIX // 4) // GLA_HEADS
GLA_DK = GLA_DV // 2
GLA_RANK = 16
GLA_TAU = 16.0
GLA_CHUNK = 64
D_FF = 256 * ((8 * D_MODEL // 3 + 255) // 256)
FFN_RES = 0.5
N_MOD = 9
EPS = 1e-6
IN_WIDTHS = (RET_HEADS * RET_DK, RET_HEADS * RET_DK, RET_HEADS * RET_DV, RET_HEADS * RET_DV, S5_WIDTH, GLA_HEADS * GLA_DK, GLA_HEADS * GLA_DK, GLA_HEADS * GLA_DV, GLA_HEADS * GLA_DV, 2 * GLA_RANK)
D_IN = sum(IN_WIDTHS)

kernel_name = "hybrid_parallel_retention_s5_gla_macaron"


def _rms_norm(x, g):
    xf = x.astype(jnp.float32)
    y = xf * lax.rsqrt(jnp.mean(xf * xf, axis=-1, keepdims=True) + EPS)
    return (y * g.astype(jnp.float32)).astype(x.dtype)


def _modulate(h, shift, scale):
    return h * (1.0 + scale[:, None, :]) + shift[:, None, :]


def _swiglu(h, w1, w3, w2):
    return (jax.nn.silu(h @ w1) * (h @ w3)) @ w2


def _head_layer_norm(o):
    mu = jnp.mean(o, axis=-1, keepdims=True)
    oc = o - mu
    return oc * lax.rsqrt(jnp.mean(oc * oc, axis=-1, keepdims=True) + EPS)


def _head_rms_norm(o):
    return o * lax.rsqrt(jnp.mean(o * o, axis=-1, keepdims=True) + EPS)


def _rotary(t, cos, sin):
    t1, t2 = jnp.split(t, 2, axis=-1)
    return jnp.concatenate([t1 * cos - t2 * sin, t1 * sin + t2 * cos], axis=-1)


def _retention_bidir(q, k, v, log_gamma):
    bsz, L, H, dk = q.shape
    dv = v.shape[-1]
    C = RET_CHUNK
    N = L // C
    q = q.reshape(bsz, N, C, H, dk) * (dk ** -0.5)
    k = k.reshape(bsz, N, C, H, dk)
    v = v.reshape(bsz, N, C, H, dv)
    idx = jnp.arange(C, dtype=jnp.float32)
    lg = log_gamma[None, :]
    dist = jnp.abs(idx[:, None] - idx[None, :])
    intra_decay = jnp.exp(dist[None] * log_gamma[:, None, None])
    scores = jnp.einsum('bnihd,bnjhd->bnhij', q, k) * intra_decay
    o = jnp.einsum('bnhij,bnjhe->bnihe', scores, v)
    k_fwd = k * jnp.exp((C - 1.0 - idx)[:, None] * lg)[:, :, None]
    k_bwd = k * jnp.exp(idx[:, None] * lg)[:, :, None]
    kv_fwd = jnp.einsum('bnjhd,bnjhe->nbhde', k_fwd, v)
    kv_bwd = jnp.einsum('bnjhd,bnjhe->nbhde', k_bwd, v)
    gamma_c = jnp.exp(C * log_gamma)[:, None, None]

    def step(s, kv):
        return gamma_c * s + kv, s

    s0 = jnp.zeros((bsz, H, dk, dv), q.dtype)
    _, s_fwd = lax.scan(step, s0, kv_fwd)
    _, s_bwd = lax.scan(step, s0, kv_bwd, reverse=True)
    q_fwd = q * jnp.exp((idx + 1.0)[:, None] * lg)[:, :, None]
    q_bwd = q * jnp.exp((C - idx)[:, None] * lg)[:, :, None]
    o = o + jnp.einsum('bnihd,nbhde->bnihe', q_fwd, s_fwd) + jnp.einsum('bnihd,nbhde->bnihe', q_bwd, s_bwd)
    return o.reshape(bsz, L, H, dv)


def _gla_causal(q, k, v, log_a, include_diag):
    bsz, L, H, dk = q.shape
    dv = v.shape[-1]
    C = GLA_CHUNK
    N = L // C
    q = q.reshape(bsz, N, C, H, dk)
    k = k.reshape(bsz, N, C, H, dk)
    log_a = log_a.reshape(bsz, N, C, H, dk)
    v = v.reshape(bsz, N, C, H, dv)
    b = jnp.cumsum(log_a, axis=2)
    b_last = b[:, :, -1]
    q_in = q * jnp.exp(b)
    k_in = k * jnp.exp(-b)
    scores = jnp.einsum('bnihd,bnjhd->bnhij', q_in, k_in)
    mask = jnp.tril(jnp.ones((C, C), dtype=bool), 0 if include_diag else -1)
    scores = jnp.where(mask, scores, 0.0)
    o = jnp.einsum('bnhij,bnjhe->bnihe', scores, v)
    k_st = k * jnp.exp(b_last[:, :, None] - b)
    kv = jnp.einsum('bnjhd,bnjhe->nbhde', k_st, v)
    decay = jnp.moveaxis(jnp.exp(b_last), 1, 0)[..., None]

    def step(s, inp):
        kv_n, a_n = inp
        return a_n * s + kv_n, s

    _, s_prev = lax.scan(step, jnp.zeros((bsz, H, dk, dv), q.dtype), (kv, decay))
    o = o + jnp.einsum('bnihd,nbhde->bnihe', q_in, s_prev)
    return o.reshape(bsz, L, H, dv)


def _gla_bidir(q, k, v, la_f, la_b):
    flip = lambda t: jnp.flip(t, axis=1)
    o_f = _gla_causal(q, k, v, la_f, True)
    o_b = flip(_gla_causal(flip(q), flip(k), flip(v), flip(la_b), False))
    return o_f + o_b


def _s5_bidir(u, lam_re, lam_im, log_dt, b_re, b_im, c_re, c_im, d_skip):
    f32 = jnp.float32
    bsz, L, W = u.shape
    ug = jnp.moveaxis(u.reshape(bsz, L, S5_GROUPS, S5_GROUP), 1, 0)
    lam = lax.complex(lam_re.astype(f32), lam_im.astype(f32))
    lam_bar = jnp.exp(lam * jnp.exp(log_dt.astype(f32))[..., None])
    b_bar = ((lam_bar - 1.0) / lam)[..., None] * lax.complex(b_re.astype(f32), b_im.astype(f32))
    c_mat = lax.complex(c_re.astype(f32), c_im.astype(f32))

    def combine(e1, e2):
        a1, x1 = e1
        a2, x2 = e2
        return a1 * a2, a2[:, None] * x1 + x2

    y = u * d_skip.astype(f32)
    for direction in range(2):
        bu = jnp.einsum('lbgc,gpc->lbgp', ug, b_bar[direction])
        a = jnp.broadcast_to(lam_bar[direction], (L,) + lam_bar.shape[1:])
        _, states = lax.associative_scan(combine, (a, bu), reverse=(direction == 1), axis=0)
        out = jnp.real(jnp.einsum('gcp,lbgp->lbgc', c_mat[direction], states))
        y = y + jnp.moveaxis(out, 0, 1).reshape(bsz, L, W)
    return y


def _hybrid_mixer(h, w_in, lam_re, lam_im, log_dt, b_re, b_im, c_re, c_im, s5_d, w_glu, b_glu, w_gate, b_gate, w_out):
    f32 = jnp.float32
    bsz, L, _ = h.shape
    pts, acc = [], 0
    for w in IN_WIDTHS[:-1]:
        acc += w
        pts.append(acc)
    proj = (h @ w_in).astype(f32)
    rq, rk, rv, rg, su, gq, gk, gv, gr, glr = jnp.split(proj, pts, axis=-1)
    pos = jnp.arange(L, dtype=f32)
    inv_freq = ROPE_BASE ** (-jnp.arange(0, RET_DK, 2, dtype=f32) / RET_DK)
    ang = pos[:, None] * inv_freq[None, :]
    cos, sin = jnp.cos(ang)[:, None, :], jnp.sin(ang)[:, None, :]
    rq = _rotary(rq.reshape(bsz, L, RET_HEADS, RET_DK), cos, sin)
    rk = _rotary(rk.reshape(bsz, L, RET_HEADS, RET_DK), cos, sin)
    rv = rv.reshape(bsz, L, RET_HEADS, RET_DV)
    log_gamma = jnp.log1p(-jnp.exp2(-5.0 - jnp.arange(RET_HEADS, dtype=f32)))
    o_ret = _retention_bidir(rq, rk, rv, log_gamma)
    y_ret = jax.nn.silu(rg) * _head_layer_norm(o_ret).reshape(bsz, L, RET_HEADS * RET_DV)
    y_s5 = _s5_bidir(su, lam_re, lam_im, log_dt, b_re, b_im, c_re, c_im, s5_d)
    z = jax.nn.gelu(y_s5)
    y_s5 = z * jax.nn.sigmoid(z @ w_glu.astype(f32) + b_glu.astype(f32))
    lr_f, lr_b = jnp.split(glr, 2, axis=-1)
    w_gate = w_gate.astype(f32)
    b_gate = b_gate.astype(f32)
    la_f = (jax.nn.log_sigmoid(lr_f @ w_gate[0] + b_gate[0]) / GLA_TAU).reshape(bsz, L, GLA_HEADS, GLA_DK)
    la_b = (jax.nn.log_sigmoid(lr_b @ w_gate[1] + b_gate[1]) / GLA_TAU).reshape(bsz, L, GLA_HEADS, GLA_DK)
    gq = gq.reshape(bsz, L, GLA_HEADS, GLA_DK) * (GLA_DK ** -0.5)
    gk = gk.reshape(bsz, L, GLA_HEADS, GLA_DK)
    gv = gv.reshape(bsz, L, GLA_HEADS, GLA_DV)
    o_gla = _gla_bidir(gq, gk, gv, la_f, la_b)
    y_gla = jax.nn.silu(gr) * _head_rms_norm(o_gla).reshape(bsz, L, GLA_HEADS * GLA_DV)
    y = jnp.concatenate([y_ret, y_s5, y_gla], axis=-1).astype(h.dtype)
    return y @ w_out


def setup_inputs(seed: int = 0) -> dict:
    key = jax.random.key(seed)
    ks = jax.random.split(key, 28)
    f32 = jnp.float32

    def nrm(k, shape, s):
        return jax.random.normal(k, shape, f32) * s

    G, P, Cg = S5_GROUPS, S5_STATE, S5_GROUP
    x = nrm(ks[0], (BATCH, SEQ, D_MODEL), 1.0)
    c = nrm(ks[1], (BATCH, D_MODEL), 1.0)
    w_ada = nrm(ks[2], (DEPTH, D_MODEL, N_MOD * D_MODEL), 0.5 * D_MODEL ** -0.5)
    b_ada = nrm(ks[3], (DEPTH, N_MOD * D_MODEL), 0.01)
    g_ffn1 = 1.0 + nrm(ks[4], (DEPTH, D_MODEL), 0.01)
    ffn1_w1 = nrm(ks[5], (DEPTH, D_MODEL, D_FF), D_MODEL ** -0.5)
    ffn1_w3 = nrm(ks[6], (DEPTH, D_MODEL, D_FF), D_MODEL ** -0.5)
    ffn1_w2 = nrm(ks[7], (DEPTH, D_FF, D_MODEL), D_FF ** -0.5)
    g_mix = 1.0 + nrm(ks[8], (DEPTH, D_MODEL), 0.01)
    w_in = nrm(ks[9], (DEPTH, D_MODEL, D_IN), D_MODEL ** -0.5)
    s5_lam_re = -0.5 + nrm(ks[10], (DEPTH, 2, G, P), 0.01)
    s5_lam_im = math.pi * jnp.arange(P, dtype=f32) + nrm(ks[11], (DEPTH, 2, G, P), 0.01)
    s5_log_dt = jax.random.uniform(ks[12], (DEPTH, 2, G), f32, math.log(1e-3), math.log(1e-1))
    s5_b_re = nrm(ks[13], (DEPTH, 2, G, P, Cg), (2 * Cg) ** -0.5)
    s5_b_im = nrm(ks[14], (DEPTH, 2, G, P, Cg), (2 * Cg) ** -0.5)
    s5_c_re = nrm(ks[15], (DEPTH, 2, G, Cg, P), P ** -0.5)
    s5_c_im = nrm(ks[16], (DEPTH, 2, G, Cg, P), P ** -0.5)
    s5_d = nrm(ks[17], (DEPTH, S5_WIDTH), 1.0)
    s5_w_glu = nrm(ks[18], (DEPTH, S5_WIDTH, S5_WIDTH), S5_WIDTH ** -0.5)
    s5_b_glu = nrm(ks[19], (DEPTH, S5_WIDTH), 0.01)
    gla_w_gate = nrm(ks[20], (DEPTH, 2, GLA_RANK, GLA_HEADS * GLA_DK), GLA_RANK ** -0.5)
    gla_b_gate = nrm(ks[21], (DEPTH, 2, GLA_HEADS * GLA_DK), 0.01)
    w_out = nrm(ks[22], (DEPTH, D_MIX, D_MODEL), D_MIX ** -0.5)
    g_ffn2 = 1.0 + nrm(ks[23], (DEPTH, D_MODEL), 0.01)
    ffn2_w1 = nrm(ks[24], (DEPTH, D_MODEL, D_FF), D_MODEL ** -0.5)
    ffn2_w3 = nrm(ks[25], (DEPTH, D_MODEL, D_FF), D_MODEL ** -0.5)
    ffn2_w2 = nrm(ks[26], (DEPTH, D_FF, D_MODEL), D_FF ** -0.5)
    g_final = 1.0 + nrm(ks[27], (D_MODEL,), 0.01)
    return {"x": x, "c": c, "w_ada": w_ada, "b_ada": b_ada, "g_ffn1": g_ffn1, "ffn1_w1": ffn1_w1, "ffn1_w3": ffn1_w3, "ffn1_w2": ffn1_w2, "g_mix": g_mix, "w_in": w_in, "s5_lam_re": s5_lam_re, "s5_lam_im": s5_lam_im, "s5_log_dt": s5_log_dt, "s5_b_re": s5_b_re, "s5_b_im": s5_b_im, "s5_c_re": s5_c_re, "s5_c_im": s5_c_im, "s5_d": s5_d, "s5_w_glu": s5_w_glu, "s5_b_glu": s5_b_glu, "gla_w_gate": gla_w_gate, "gla_b_gate": gla_b_gate, "w_out": w_out, "g_ffn2": g_ffn2, "ffn2_w1": ffn2_w1, "ffn2_w3": ffn2_w3, "ffn2_w2": ffn2_w2, "g_final": g_final}


def reference(x, c, w_ada, b_ada, g_ffn1, ffn1_w1, ffn1_w3, ffn1_w2, g_mix, w_in, s5_lam_re, s5_lam_im, s5_log_dt, s5_b_re, s5_b_im, s5_c_re, s5_c_im, s5_d, s5_w_glu, s5_b_glu, gla_w_gate, gla_b_gate, w_out, g_ffn2, ffn2_w1, ffn2_w3, ffn2_w2, g_final):
    h = x
    cond = jax.nn.silu(c)
    for l in range(DEPTH):
        mod = cond @ w_ada[l] + b_ada[l]
        sh1, sc1, gt1, sh2, sc2, gt2, sh3, sc3, gt3 = jnp.split(mod, N_MOD, axis=-1)
        u = _modulate(_rms_norm(h, g_ffn1[l]), sh1, sc1)
        h = h + FFN_RES * gt1[:, None, :] * _swiglu(u, ffn1_w1[l], ffn1_w3[l], ffn1_w2[l])
        u = _modulate(_rms_norm(h, g_mix[l]), sh2, sc2)
        h = h + gt2[:, None, :] * _hybrid_mixer(u, w_in[l], s5_lam_re[l], s5_lam_im[l], s5_log_dt[l], s5_b_re[l], s5_b_im[l], s5_c_re[l], s5_c_im[l], s5_d[l], s5_w_glu[l], s5_b_glu[l], gla_w_gate[l], gla_b_gate[l], w_out[l])
        u = _modulate(_rms_norm(h, g_ffn2[l]), sh3, sc3)
        h = h + FFN_RES * gt3[:, None, :] * _swiglu(u, ffn2_w1[l], ffn2_w3[l], ffn2_w2[l])
    return _rms_norm(h, g_final)
```

```python
import contextlib, math, os
import numpy as np
import concourse.bass as bass
import concourse.mybir as mybir
from concourse.bass_utils import run_bass_kernel_spmd


ENGS = ("pe", "act", "dve", "pool", "sp")
N_DMA_SEMS = 6


class Prog:
    def __init__(self, nc):
        self.nc = nc
        self.ops = {e: [] for e in ENGS}
        self.cnt = {e: 0 for e in ENGS}
        self.pending = {e: False for e in ENGS}
        self.seen = {e: {} for e in ENGS}
        self.state = {}
        self.dma_cnt = {}
        self.dma_rr = {e: 0 for e in ENGS}
        self.n_ops = 0

    def _deps(self, eng, reads, writes):
        need = {}
        def add(tok, raw):
            if tok is None:
                return
            sk, v = tok
            if sk == eng and not raw:
                return
            if sk == eng and eng == "pe":
                return
            if need.get(sk, 0) < v:
                need[sk] = v
        for k in reads:
            st = self.state.get(k)
            if st:
                add(st[0], True)
                if isinstance(k, tuple) and k[0] in ("pb", "pt"):
                    for sk, v in st[1].items():
                        add((sk, v), False)
        for k in writes:
            st = self.state.get(k)
            if st:
                add(st[0], False)
                for sk, v in st[1].items():
                    add((sk, v), False)
        out = []
        for sk, v in need.items():
            if self.seen[eng].get(sk, 0) < v:
                self.seen[eng][sk] = v
                out.append((sk, v))
        return out

    def _commit(self, tok, reads, writes):
        for k in reads:
            st = self.state.setdefault(k, [None, {}])
            sk, v = tok
            if st[1].get(sk, 0) < v:
                st[1][sk] = v
        for k in writes:
            self.state[k] = [tok, {}]

    def op(self, eng, fn, reads=(), writes=(), inc=True):
        waits = self._deps(eng, reads, writes)
        if inc:
            self.cnt[eng] += 1
            tok = (eng, self.cnt[eng])
            self.pending[eng] = False
        else:
            tok = (eng, self.cnt[eng] + 1)
            self.pending[eng] = True
        self.ops[eng].append((waits, fn, eng if inc else None))
        self._commit(tok, reads, writes)
        self.n_ops += 1
        return tok

    def dma(self, q, out, in_, reads=(), writes=(), **kw):
        waits = self._deps(q, reads, writes)
        i = self.dma_rr[q]
        self.dma_rr[q] = (i + 1) % N_DMA_SEMS
        sk = ("d", q, i)
        self.dma_cnt[sk] = self.dma_cnt.get(sk, 0) + 16
        tok = (sk, self.dma_cnt[sk])
        self.ops[q].append((waits, lambda e, o=out, n=in_, kw=kw: e.dma_start(out=o, in_=n, **kw), sk))
        self._commit(tok, reads, writes)
        self.n_ops += 1
        return tok

    def barrier(self):
        assert not any(self.pending.values()), self.pending
        allt = [(sk, v) for sk, v in list(self.cnt.items()) + list(self.dma_cnt.items()) if v > 0]
        for e in ENGS:
            waits = []
            for sk, v in allt:
                if sk == e:
                    continue
                if self.seen[e].get(sk, 0) < v:
                    self.seen[e][sk] = v
                    waits.append((sk, v))
            if waits:
                self.ops[e].append((waits, None, None))
        self.state = {}

    def emit(self, final_waits_eng="sp"):
        nc = self.nc
        assert not any(self.pending.values()), self.pending
        with contextlib.ExitStack() as es:
            sems = {}
            for e in ENGS:
                sems[e] = es.enter_context(nc.semaphore("s_" + e))
            for q in ("sp", "act", "pool"):
                for i in range(N_DMA_SEMS):
                    sems[("d", q, i)] = es.enter_context(nc.semaphore("d_%s_%d" % (q, i)))
            block = es.enter_context(nc.Block())
            fin = []
            for sk, v in list(self.cnt.items()) + list(self.dma_cnt.items()):
                if v > 0 and sk != final_waits_eng:
                    fin.append((sk, v))

            def run(eng_name):
                def body(e):
                    for waits, fn, inc in self.ops[eng_name]:
                        for sk, v in waits:
                            e.wait_ge(sems[sk], v)
                        if fn is None:
                            continue
                        ins = fn(e)
                        if inc is not None:
                            ins.then_inc(sems[inc], 16 if isinstance(inc, tuple) else 1)
                    if eng_name == final_waits_eng:
                        for sk, v in fin:
                            e.wait_ge(sems[sk], v)
                return body

            block.tensor(run("pe"))
            block.scalar(run("act"))
            block.vector(run("dve"))
            block.gpsimd(run("pool"))
            block.sync(run("sp"))


F32 = mybir.dt.float32
BF16 = mybir.dt.bfloat16
AF = mybir.ActivationFunctionType
ALU = mybir.AluOpType
TWO_PI = 2.0 * math.pi


def s5_phase(B, l):
    P = B.P; T = B.T; sbt = B.sbt; U = B.U; bank = B.bank
    suT = B.suT; yT = B.yT
    NC5 = T // 512
    NLEV = int(round(math.log2(T)))
    assert 2 ** NLEV == T

    def tt_op(eng, out, in0, in1, op, reads, writes):
        P.op(eng, lambda e: e.tensor_tensor(out=out, in0=in0, in1=in1, op=op), reads=reads, writes=writes)

    def chunk(c, rev):
        if not rev:
            return slice(c * 512, (c + 1) * 512)
        hi = T - 1 - c * 512
        lo = T - 1 - (c + 1) * 512
        return slice(hi, None if lo < 0 else lo, -1)

    with contextlib.ExitStack() as es:
        par = sbt(es, U("s5par"), [128, 2, 3, 16], F32)
        P.dma("sp", par[:], B.s5p[l].rearrange("d p v s -> p d v s"), writes=["par"])
        sc = sbt(es, U("s5sc"), [128, 20, 2, 16], F32)
        SL = {nm: i for i, nm in enumerate(["dt", "redt", "th", "rho", "k1", "k2", "r", "sin", "cos", "lr", "li", "nr", "den", "t1", "t2", "cre", "cim", "ncim", "th2", "k3"])}
        S = lambda nm: sc[:, SL[nm]]
        K = lambda nm: ("s5sc", nm)

        def ts(out, in0, s1, op0, reads, writes, s2=None, op1=None):
            if op1 is None:
                P.op("dve", lambda e: e.tensor_scalar(out=out, in0=in0, scalar1=s1, scalar2=None, op0=op0), reads=reads, writes=writes)
            else:
                P.op("dve", lambda e: e.tensor_scalar(out=out, in0=in0, scalar1=s1, scalar2=s2, op0=op0, op1=op1), reads=reads, writes=writes)

        P.op("act", lambda e: e.activation(out=S("dt"), in_=par[:, :, 2, :], func=AF.Exp), reads=["par"], writes=[K("dt")])
        tt_op("dve", S("redt"), par[:, :, 0, :], S("dt"), ALU.mult, ["par", K("dt")], [K("redt")])
        tt_op("dve", S("th"), par[:, :, 1, :], S("dt"), ALU.mult, ["par", K("dt")], [K("th")])
        P.op("act", lambda e: e.activation(out=S("rho"), in_=S("redt"), func=AF.Exp), reads=[K("redt")], writes=[K("rho")])

        def sin_of(src_nm, dst_nm):
            ts(S("k1"), S(src_nm), TWO_PI, ALU.is_ge, [K(src_nm)], [K("k1")])
            ts(S("k2"), S(src_nm), 2 * TWO_PI, ALU.is_ge, [K(src_nm)], [K("k2")])
            ts(S("k3"), S(src_nm), 3 * TWO_PI, ALU.is_ge, [K(src_nm)], [K("k3")])
            tt_op("dve", S("k1"), S("k1"), S("k2"), ALU.add, [K("k1"), K("k2")], [K("k1")])
            tt_op("dve", S("k1"), S("k1"), S("k3"), ALU.add, [K("k1"), K("k3")], [K("k1")])
            P.op("dve", lambda e: e.scalar_tensor_tensor(out=S("r"), in0=S("k1"), scalar=-TWO_PI, in1=S(src_nm), op0=ALU.mult, op1=ALU.add), reads=[K("k1"), K(src_nm)], writes=[K("r")])
            ts(S("r"), S("r"), 0.0, ALU.max, [K("r")], [K("r")], s2=TWO_PI, op1=ALU.min)
            P.op("act", lambda e: e.activation(out=S(dst_nm), in_=S("r"), func=AF.Sin, scale=-1.0, bias=B.pic[:]), reads=[K("r"), "pic"], writes=[K(dst_nm)])

        sin_of("th", "sin")
        ts(S("th2"), S("th"), math.pi / 2, ALU.add, [K("th")], [K("th2")])
        sin_of("th2", "cos")
        tt_op("dve", S("lr"), S("rho"), S("cos"), ALU.mult, [K("rho"), K("cos")], [K("lr")])
        tt_op("dve", S("li"), S("rho"), S("sin"), ALU.mult, [K("rho"), K("sin")], [K("li")])
        ts(S("nr"), S("lr"), -1.0, ALU.add, [K("lr")], [K("nr")])
        tt_op("dve", S("den"), par[:, :, 0, :], par[:, :, 0, :], ALU.mult, ["par"], [K("den")])
        tt_op("dve", S("t1"), par[:, :, 1, :], par[:, :, 1, :], ALU.mult, ["par"], [K("t1")])
        tt_op("dve", S("den"), S("den"), S("t1"), ALU.add, [K("den"), K("t1")], [K("den")])
        P.op("dve", lambda e: e.reciprocal(out=S("den"), in_=S("den")), reads=[K("den")], writes=[K("den")])
        tt_op("dve", S("t1"), S("nr"), par[:, :, 0, :], ALU.mult, [K("nr"), "par"], [K("t1")])
        tt_op("dve", S("t2"), S("li"), par[:, :, 1, :], ALU.mult, [K("li"), "par"], [K("t2")])
        tt_op("dve", S("t1"), S("t1"), S("t2"), ALU.add, [K("t1"), K("t2")], [K("t1")])
        tt_op("dve", S("cre"), S("t1"), S("den"), ALU.mult, [K("t1"), K("den")], [K("cre")])
        tt_op("dve", S("t1"), S("li"), par[:, :, 0, :], ALU.mult, [K("li"), "par"], [K("t1")])
        tt_op("dve", S("t2"), S("nr"), par[:, :, 1, :], ALU.mult, [K("nr"), "par"], [K("t2")])
        tt_op("dve", S("t1"), S("t1"), S("t2"), ALU.subtract, [K("t1"), K("t2")], [K("t1")])
        tt_op("dve", S("cim"), S("t1"), S("den"), ALU.mult, [K("t1"), K("den")], [K("cim")])
        ts(S("ncim"), S("cim"), -1.0, ALU.mult, [K("cim")], [K("ncim")])

        su_ct = sbt(es, U("su_ct"), [128, T], BF16)
        dcol = sbt(es, U("s5dcol"), [128, 4], F32)
        P.dma("sp", dcol[:], B.s5d[l], writes=["s5dcol"])
        bgl = sbt(es, U("bgl"), [128, 4], F32)
        P.dma("sp", bgl[:], B.bglu[l], writes=["bgl"])
        wgl = sbt(es, U("wgl"), [128, 4, 512], BF16)
        P.dma("pool", wgl[:], B.wglu[l].rearrange("(k p) n -> p k n", p=128), writes=["wgl"])
        TC = min(T, 1024)
        sA = sbt(es, U("sA"), [128, TC], F32)
        sI = sbt(es, U("sI"), [128, TC], mybir.dt.int32)
        hB = sbt(es, U("hB"), [128, TC], F32)
        U1 = sbt(es, U("U1"), [128, 32, T // 64], F32)
        U2 = sbt(es, U("U2"), [128, 32, 64], F32)
        U1i = sbt(es, U("U1i"), [128, 32, 64], mybir.dt.int32)
        io64 = sbt(es, U("io64"), [128, 64], F32)
        P.dma("sp", io64[:], B.cin["iota64"][:], writes=["io64"])
        fq = sc[:, SL["k2"]].rearrange("p d s -> p (d s)"); gq = sc[:, SL["k3"]].rearrange("p d s -> p (d s)")
        ts(S("k2"), S("th"), 1.0 / TWO_PI, ALU.mult, [K("th")], [K("k2")])
        ts(S("k3"), S("k2"), 64.0, ALU.mult, [K("k2")], [K("k3")])
        P.op("dve", lambda e: e.tensor_copy(out=U1i[:, :, 0], in_=gq), reads=[K("k3")], writes=["U1i"])
        tt_op("dve", gq, gq, U1i[:, :, 0], ALU.subtract, [K("k3"), "U1i"], [K("k3")])
        NA = T // 64
        tt_op("dve", U1[:], gq.unsqueeze(2).broadcast_to([128, 32, NA]), io64[:, 0:NA].unsqueeze(1).broadcast_to([128, 32, NA]), ALU.mult, [K("k3"), "io64"], ["U1"])
        P.op("dve", lambda e: e.tensor_copy(out=U1i[:, :, 0:NA], in_=U1[:]), reads=["U1"], writes=["U1i"])
        tt_op("dve", U1[:], U1[:], U1i[:, :, 0:NA], ALU.subtract, ["U1", "U1i"], ["U1"])
        tt_op("dve", U2[:], fq.unsqueeze(2).broadcast_to([128, 32, 64]), io64[:].unsqueeze(1).broadcast_to([128, 32, 64]), ALU.mult, [K("k2"), "io64"], ["U2"])
        P.op("dve", lambda e: e.tensor_copy(out=U1i[:], in_=U2[:]), reads=["U2"], writes=["U1i"])
        tt_op("dve", U2[:], U2[:], U1i[:], ALU.subtract, ["U2", "U1i"], ["U2"])
        zz = sbt(es, U("zz"), [128, 2, T], BF16)
        csbs = [sbt(es, U("csb"), [128, 2, T], BF16) for _ in range(2)]
        POOL_LAST = os.environ.get("S5_POOL_LAST", "0") == "1"
        RHO_BCAST = os.environ.get("S5_RHO_BCAST", "1") == "1"
        if not RHO_BCAST:
            rho_t = sbt(es, U("rho_t"), [128, T], F32)
        xx = sbt(es, U("xx"), [128, 2, T], BF16)
        y_ct = sbt(es, U("y_ct"), [128, T], F32)
        zT = sbt(es, U("zT"), [128, 4, T], BF16)
        Bb = [sbt(es, U("Bb"), [128, 2, 128], BF16) for _ in range(2)]
        Cf = [sbt(es, U("Cf"), [128, 2, 128], F32) for _ in range(2)]
        Cp = [sbt(es, U("Cp"), [128, 2, 128], BF16) for _ in range(2)]
        ctmp = sbt(es, U("ctmp"), [128, 128], F32)
        tmp = [sbt(es, U("s5tmp"), [128, 512], F32) for _ in range(4)]
        tmpb = [sbt(es, U("s5tmpb"), [128, 512], BF16) for _ in range(12)]
        tbi = [0]

        def TB_():
            i = tbi[0] % 12; tbi[0] += 1
            return tmpb[i], ("s5tmpb", i)

        ti = [0]

        def T_():
            i = ti[0] % 4; ti[0] += 1
            return tmp[i], ("s5tmp", i)

        def gen_tables(d, st, csb, ckey):
            dsi = d * 16 + st
            for h2 in range(T // TC):
                a0 = h2 * (TC // 64); a1 = (h2 + 1) * (TC // 64)
                na = a1 - a0
                sA3 = sA[:].rearrange("p (a b) -> p a b", b=64)
                tt_op("dve", sA3, U1[:, dsi, a0:a1].unsqueeze(2).broadcast_to([128, na, 64]), U2[:, dsi, :].unsqueeze(1).broadcast_to([128, na, 64]), ALU.add, ["U1", "U2"], ["sA"])
                P.op("dve", lambda e: e.tensor_copy(out=sI[:], in_=sA[:]), reads=["sA"], writes=["sI"])
                tt_op("dve", sA[:], sA[:], sI[:], ALU.subtract, ["sA", "sI"], ["sA"])
                tsl = slice(h2 * TC, (h2 + 1) * TC)
                P.op("act", lambda e, tsl=tsl, csb=csb: e.activation(out=csb[:, 1, tsl], in_=sA[:], func=AF.Sin, scale=TWO_PI), reads=["sA"], writes=[ckey])
                P.op("act", lambda e: e.activation(out=hB[:], in_=sA[:], func=AF.Sin, scale=math.pi), reads=["sA"], writes=["hB"])
                P.op("act", lambda e: e.activation(out=hB[:], in_=hB[:], func=AF.Square), reads=["hB"], writes=["hB"])
                P.op("act", lambda e, tsl=tsl, csb=csb: e.activation(out=csb[:, 0, tsl], in_=hB[:], func=AF.Identity, scale=-2.0, bias=1.0), reads=["hB"], writes=[ckey])
            pass

        iters = [(ct_, d_, st_) for ct_ in range(4) for d_ in range(2) for st_ in range(ct_ * 4, ct_ * 4 + 4)]
        gen_tables(iters[0][1], iters[0][2], csbs[0], ("csb", 0))
        it = 0
        for ct in range(4):
            P.dma("sp", su_ct[:], suT[ct], reads=[("suT", ct, c) for c in range(NC5)], writes=["su_all"])
            ts(y_ct[:], su_ct[:], dcol[:, ct:ct + 1], ALU.mult, ["su_all", "s5dcol"], ["y_ct"])
            for d in range(2):
                rev = (d == 1)
                for st in range(ct * 4, ct * 4 + 4):
                    b = it % 2; it += 1
                    P.dma("pool", Bb[b][:], B.s5B[l, d, :, st].rearrange("r k m -> k r m"), writes=[("Bb", b)])
                    P.dma("sp", Cf[b][:], B.s5C[l, d, :, st].rearrange("r k m -> k r m"), writes=[("Cf", b)])
                    cre = sc[:, SL["cre"], d, st:st + 1]; cim = sc[:, SL["cim"], d, st:st + 1]; ncim = sc[:, SL["ncim"], d, st:st + 1]
                    ts(ctmp[:], Cf[b][:, 1, :], cim, ALU.mult, [("Cf", b), K("cim")], ["ctmp"])
                    P.op("dve", lambda e, b=b, cre=cre: e.scalar_tensor_tensor(out=Cp[b][:, 0, :], in0=Cf[b][:, 0, :], scalar=cre, in1=ctmp[:], op0=ALU.mult, op1=ALU.subtract), reads=[("Cf", b), K("cre"), "ctmp"], writes=[("Cp", b, 0)])
                    ts(ctmp[:], Cf[b][:, 1, :], cre, ALU.mult, [("Cf", b), K("cre")], ["ctmp"])
                    P.op("dve", lambda e, b=b, ncim=ncim: e.scalar_tensor_tensor(out=Cp[b][:, 1, :], in0=Cf[b][:, 0, :], scalar=ncim, in1=ctmp[:], op0=ALU.mult, op1=ALU.subtract), reads=[("Cf", b), K("ncim"), "ctmp"], writes=[("Cp", b, 1)])
                    csb = csbs[(it - 1) % 2]; ckey = ("csb", (it - 1) % 2)
                    csk = []
                    rho_col = sc[:, SL["rho"], d, st:st + 1]
                    if RHO_BCAST:
                        rho_ap = rho_col.broadcast_to([128, T]); rho_rd = [K("rho")]
                    else:
                        P.op("act", lambda e, rho_col=rho_col: e.activation(out=rho_t[:], in_=zz[:, 0, :], func=AF.Identity, scale=0.0, bias=rho_col), reads=[K("rho")], writes=["rho_t"])
                        rho_ap = rho_t[:]; rho_rd = ["rho_t"]
                    for c in range(NC5):
                        sl = slice(c * 512, (c + 1) * 512)
                        pA, kAp = bank(); pB, kBp = bank()
                        rhs = su_ct[:, chunk(c, rev)]
                        P.op("pe", lambda e, pA=pA, b=b, rhs=rhs: e.matmul(pA[:], lhsT=Bb[b][:, 0, :], rhs=rhs, start=True, stop=True), reads=[("Bb", b), "su_all"], writes=[kAp])
                        P.op("pe", lambda e, pB=pB, b=b, rhs=rhs: e.matmul(pB[:], lhsT=Bb[b][:, 1, :], rhs=rhs, start=True, stop=True), reads=[("Bb", b), "su_all"], writes=[kBp])
                        ab, kab = TB_(); bb_, kbb = TB_()
                        P.op("act", lambda e, pA=pA, ab=ab: e.copy(out=ab[:], in_=pA[:]), reads=[kAp], writes=[kab])
                        P.op("act", lambda e, pB=pB, bb_=bb_: e.copy(out=bb_[:], in_=pB[:]), reads=[kBp], writes=[kbb])
                        t1, k1 = TB_(); t2, k2 = TB_(); t3, k3 = TB_(); t4, k4 = TB_()
                        tt_op("dve", t1[:], ab[:], csb[:, 0, sl], ALU.mult, [kab, ckey], [k1])
                        tt_op("pool", t2[:], bb_[:], csb[:, 1, sl], ALU.mult, [kbb, ckey], [k2])
                        tt_op("dve", zz[:, 0, sl], t1[:], t2[:], ALU.add, [k1, k2], [("zz", 0, c)])
                        tt_op("pool", t3[:], bb_[:], csb[:, 0, sl], ALU.mult, [kbb, ckey], [k3])
                        tt_op("dve", t4[:], ab[:], csb[:, 1, sl], ALU.mult, [kab, ckey], [k4])
                        tt_op("dve", zz[:, 1, sl], t3[:], t4[:], ALU.subtract, [k3, k4], [("zz", 1, c)])
                    if it < len(iters):
                        gen_tables(iters[it][1], iters[it][2], csbs[it % 2], ("csb", it % 2))
                    for ri in range(2):
                        zk = [("zz", ri, c) for c in range(NC5)]
                        P.op("dve", lambda e, ri=ri, rho_ap=rho_ap: e.tensor_tensor_scan(out=zz[:, ri, :], data0=rho_ap, data1=zz[:, ri, :], initial=0.0, op0=ALU.mult, op1=ALU.add), reads=zk + rho_rd, writes=zk)
                    for c in range(NC5):
                        sl = slice(c * 512, (c + 1) * 512)
                        t1, k1 = TB_(); t2, k2 = TB_(); t3, k3 = TB_(); t4, k4 = TB_()
                        tt_op("dve", t1[:], zz[:, 0, sl], csb[:, 0, sl], ALU.mult, [("zz", 0, c), ckey], [k1])
                        tt_op("pool", t2[:], zz[:, 1, sl], csb[:, 1, sl], ALU.mult, [("zz", 1, c), ckey], [k2])
                        tt_op("dve", xx[:, 0, sl], t1[:], t2[:], ALU.subtract, [k1, k2], [("xx", c)])
                        tt_op("pool", t3[:], zz[:, 1, sl], csb[:, 0, sl], ALU.mult, [("zz", 1, c), ckey], [k3])
                        tt_op("dve", t4[:], zz[:, 0, sl], csb[:, 1, sl], ALU.mult, [("zz", 0, c), ckey], [k4])
                        tt_op("pool", xx[:, 1, sl], t3[:], t4[:], ALU.add, [k3, k4], [("xx", c)])
                    for c in range(NC5):
                        cc = c if not rev else NC5 - 1 - c
                        pY, kY = bank()
                        sl = slice(c * 512, (c + 1) * 512)
                        if not rev:
                            r_re = xx[:, 0, sl]; r_im = xx[:, 1, sl]
                        else:
                            r_re = xx[:, 0, chunk(c, True)]; r_im = xx[:, 1, chunk(c, True)]
                        P.op("pe", lambda e, pY=pY, b=b, r_re=r_re: e.matmul(pY[:], lhsT=Cp[b][:, 0, :], rhs=r_re, start=True, stop=False), reads=[("Cp", b, 0), ("xx", cc)], writes=[kY], inc=False)
                        P.op("pe", lambda e, pY=pY, b=b, r_im=r_im: e.matmul(pY[:], lhsT=Cp[b][:, 1, :], rhs=r_im, start=False, stop=True), reads=[("Cp", b, 1), ("xx", cc)], writes=[kY])
                        tt_op("dve", y_ct[:, sl], pY[:], y_ct[:, sl], ALU.add, [kY, "y_ct"], ["y_ct"])
            for c in range(NC5):
                sl = slice(c * 512, (c + 1) * 512)
                t1, k1 = T_(); t2, k2 = T_()
                tt_op("dve", t1[:], y_ct[:, sl], y_ct[:, sl], ALU.mult, ["y_ct"], [k1])
                ts(t1[:], t1[:], 0.044715, ALU.mult, [k1], [k1], s2=1.0, op1=ALU.add)
                tt_op("dve", t2[:], t1[:], y_ct[:, sl], ALU.mult, [k1, "y_ct"], [k2])
                P.op("act", lambda e, t2=t2: e.activation(out=t2[:], in_=t2[:], func=AF.Sigmoid, scale=2.0 * math.sqrt(2.0 / math.pi)), reads=[k2], writes=[k2])
                tt_op("dve", zT[:, ct, sl], t2[:], y_ct[:, sl], ALU.mult, [k2, "y_ct"], [("zT", ct, c)])
        ystg = [sbt(es, U("s5ystg"), [128, 512], BF16) for _ in range(2)]
        yi = 0
        for m in range(4):
            for c in range(NC5):
                sl = slice(c * 512, (c + 1) * 512)
                pb, pk = bank()
                for kt in range(4):
                    P.op("pe", lambda e, pb=pb, kt=kt, m=m, sl=sl: e.matmul(pb[:], lhsT=wgl[:, kt, m * 128:(m + 1) * 128], rhs=zT[:, kt, sl], start=(kt == 0), stop=(kt == 3)),
                         reads=["wgl"] + [("zT", kt, c)], writes=[pk], inc=(kt == 3))
                t1, k1 = T_()
                P.op("act", lambda e, pb=pb, t1=t1, m=m: e.activation(out=t1[:], in_=pb[:], func=AF.Sigmoid, bias=bgl[:, m:m + 1], scale=1.0), reads=[pk, "bgl"], writes=[k1])
                ys = ystg[yi % 2]; yk = ("s5ystg", yi % 2); yi += 1
                tt_op("dve", ys[:], t1[:], zT[:, m, sl], ALU.mult, [k1, ("zT", m, c)], [yk])
                P.dma("sp", yT[8 + m, :, sl], ys[:], reads=[yk], writes=[("yT", 1, c * 4 + j) for j in range(4)])


F32 = mybir.dt.float32
BF16 = mybir.dt.bfloat16
AF = mybir.ActivationFunctionType
ALU = mybir.AluOpType
AX = mybir.AxisListType
KT = 16
D = 2048
EPS = 1e-6
LOG_GAMMA = [math.log1p(-2.0 ** (-5.0 - h)) for h in range(4)]
GAMMA_C = [math.exp(128.0 * g) for g in LOG_GAMMA]
TWO_PI = 2.0 * math.pi


def mixer(B, l):
    P = B.P; nc = B.nc; T = B.T; NT = B.NT; TT = B.TT; NTT = B.NTT; NCH = B.NCH
    sbt = B.sbt; U = B.U; bank = B.bank; bank2 = B.bank2; bankb = B.bankb
    hT = B.hT; proj = B.proj; suT = B.suT; glrT = B.glrT; yT = B.yT
    ident = B.ident; onesf = B.onesf; epsc = B.epsc; g05 = B.g05
    cin = B.cin
    NC5 = T // 512
    alt = [0]

    def evac(out_ap, in_ap, reads, writes):
        alt[0] ^= 1
        if alt[0]:
            P.op("act", lambda e: e.copy(out=out_ap, in_=in_ap), reads=reads, writes=writes)
        else:
            P.op("dve", lambda e: e.tensor_copy(out=out_ap, in_=in_ap), reads=reads, writes=writes)

    def tt_op(eng, out, in0, in1, op, reads, writes):
        P.op(eng, lambda e: e.tensor_tensor(out=out, in0=in0, in1=in1, op=op), reads=reads, writes=writes)

    def transposes(src_fn, nblk, dst, dkey, reads):
        pt, pk = bankb()
        for i in range(nblk):
            P.op("pe", lambda e, i=i: e.transpose(out=pt[:, i * 128:(i + 1) * 128], in_=src_fn(i), identity=ident[:]), reads=list(reads) + ["ident"], writes=[pk], inc=(i == nblk - 1))
        return pt, pk

    with contextlib.ExitStack() as es:
        uT = sbt(es, U("muT"), [128, KT, TT], BF16)
        hch = sbt(es, U("mhch"), [128, KT, 512], F32)
        sq = sbt(es, U("msq"), [128, KT, 512], BF16)
        rstd = sbt(es, U("mrstd"), [128, 512], F32)
        wch = [sbt(es, U("wch"), [128, KT, 512], BF16) for _ in range(2)]
        wg32 = sbt(es, U("wg32"), [128, KT, 32], BF16)
        stg = [sbt(es, U("stg"), [128, 512], BF16) for _ in range(4)]
        stgf = [sbt(es, U("stgf"), [32, 512], F32) for _ in range(2)]
        P.dma("pool", wg32[:], B.w_in[l, :, 5120:5152].rearrange("(k p) n -> p k n", p=128), writes=["wg32"])
        wi = 0; si = 0; sfi = 0
        for tt in range(NTT):
            t0 = tt * TT
            B.norm_mod((hch, sq, rstd), 1, t0, TT, uT, "uT")
            for cc in range(10):
                wb = wch[wi % 2]; wk = ("wch", wi % 2); wi += 1
                P.dma("pool", wb[:], B.w_in[l, :, cc * 512:(cc + 1) * 512].rearrange("(k p) n -> p k n", p=128), writes=[wk])
                for n in range(TT // 128):
                    pb, pk = bank()
                    for kt in range(KT):
                        P.op("pe", lambda e, pb=pb, kt=kt, n=n, wb=wb: e.matmul(pb[:], lhsT=uT[:, kt, n * 128:(n + 1) * 128], rhs=wb[:, kt, :], start=(kt == 0), stop=(kt == KT - 1)),
                             reads=["uT", wk], writes=[pk], inc=(kt == KT - 1))
                    sb_ = stg[si % 4]; sk = ("stg", si % 4); si += 1
                    evac(sb_[:], pb[:], [pk], [sk])
                    r0 = t0 + n * 128
                    P.dma("sp", proj[r0:r0 + 128, cc * 512:(cc + 1) * 512], sb_[:], reads=[sk], writes=[("proj", r0 // 128, cc)])
                if cc == 6:
                    for m in range(4):
                        for c in range(NCH):
                            pb, pk = bank()
                            for kt in range(KT):
                                P.op("pe", lambda e, pb=pb, kt=kt, m=m, c=c, wb=wb: e.matmul(pb[:], lhsT=wb[:, kt, m * 128:(m + 1) * 128], rhs=uT[:, kt, c * 512:(c + 1) * 512], start=(kt == 0), stop=(kt == KT - 1)),
                                     reads=["uT", wk], writes=[pk], inc=(kt == KT - 1))
                            sb_ = stg[si % 4]; sk = ("stg", si % 4); si += 1
                            evac(sb_[:], pb[:], [pk], [sk])
                            c0 = t0 + c * 512
                            P.dma("sp", suT[m, :, c0:c0 + 512], sb_[:], reads=[sk], writes=[("suT", m, c0 // 512)])
            for c in range(NCH):
                pb, pk = bank()
                for kt in range(KT):
                    P.op("pe", lambda e, pb=pb, kt=kt, c=c: e.matmul(pb[0:32, :], lhsT=wg32[:, kt, :], rhs=uT[:, kt, c * 512:(c + 1) * 512], start=(kt == 0), stop=(kt == KT - 1)),
                         reads=["uT", "wg32"], writes=[pk], inc=(kt == KT - 1))
                sf = stgf[sfi % 2]; sfk = ("stgf", sfi % 2); sfi += 1
                evac(sf[:], pb[0:32, :], [pk], [sfk])
                c0 = t0 + c * 512
                P.dma("sp", glrT[:, c0:c0 + 512], sf[:], reads=[sfk], writes=[("glrT", c0 // 512)])
    P.barrier()

    SKIP = os.environ.get("MIXSKIP", "").split(",")
    def r_gen(es):
     if True:
      if "R" not in SKIP:
            ropec = sbt(es, U("ropec"), [128, NT, 64], F32)
            ropes = sbt(es, U("ropes"), [128, NT, 64], F32)
            rmask = sbt(es, U("rmask"), [128, 4, 128], F32)
            ftab = sbt(es, U("ftab"), [128, 4, 128], F32)
            btab = sbt(es, U("btab"), [128, 4, 128], F32)
            kfb = sbt(es, U("kfb"), [128, 2, 4], F32)
            for nm, tl in (("ropec", ropec), ("ropes", ropes), ("rmask", rmask), ("ftab", ftab), ("btab", btab), ("kfb", kfb)):
                P.dma("sp", tl[:], cin[nm][:], writes=[nm])
            Sb_all = sbt(es, U("Sb_all"), [128, NT, 4, 256], BF16)
            Srun = sbt(es, U("Srun"), [128, 4, 256], F32)
            Sbf = sbt(es, U("Sbf"), [128, 4, 256], BF16)
            qk = [sbt(es, U("qk"), [128, 8, 128], BF16) for _ in range(2)]
            vv = [sbt(es, U("vv"), [128, 1024], BF16) for _ in range(2)]
            rg = [sbt(es, U("rg"), [128, 1024], BF16) for _ in range(2)]
            ra = sbt(es, U("ra"), [128, 8, 64], F32)
            rb = sbt(es, U("rb"), [128, 8, 64], F32)
            qkr = sbt(es, U("qkr"), [128, 8, 128], BF16)
            ksc = sbt(es, U("ksc"), [128, 4, 128], BF16)
            qT = sbt(es, U("qT"), [128, 4, 128], BF16)
            kT = sbt(es, U("kT"), [128, 4, 128], BF16)
            qfT = sbt(es, U("qfT"), [128, 4, 128], BF16)
            qbT = sbt(es, U("qbT"), [128, 4, 128], BF16)
            sT = sbt(es, U("sT"), [128, 4, 128], BF16)
            sqs = sbt(es, U("sqs"), [128, 1024], F32)
            yn = sbt(es, U("yn"), [128, 1024], F32)
            sg = sbt(es, U("sg"), [128, 1024], F32)
            yg = sbt(es, U("yg"), [128, 8, 128], BF16)
            ystg = [sbt(es, U("ystg"), [128, 8, 128], BF16) for _ in range(2)]
            st4 = sbt(es, U("st4"), [128, 6, 4], F32)

            def rope(src, nh, n, key_src):
                cs = ropec[:, n, :].unsqueeze(1).broadcast_to([128, nh, 64])
                sn = ropes[:, n, :].unsqueeze(1).broadcast_to([128, nh, 64])
                t1 = src[:, :, 0:64]; t2 = src[:, :, 64:128]
                tt_op("dve", ra[:, 0:nh, :], t1, cs, ALU.mult, [key_src, "ropec"], ["ra"])
                tt_op("dve", rb[:, 0:nh, :], t2, sn, ALU.mult, [key_src, "ropes"], ["rb"])
                tt_op("dve", qkr[:, 0:nh, 0:64], ra[:, 0:nh, :], rb[:, 0:nh, :], ALU.subtract, ["ra", "rb"], ["qkr1"])
                tt_op("dve", ra[:, 0:nh, :], t1, sn, ALU.mult, [key_src, "ropes"], ["ra"])
                tt_op("dve", rb[:, 0:nh, :], t2, cs, ALU.mult, [key_src, "ropec"], ["rb"])
                tt_op("dve", qkr[:, 0:nh, 64:128], ra[:, 0:nh, :], rb[:, 0:nh, :], ALU.add, ["ra", "rb"], ["qkr2"])

            def kv_update(n, ksrc_ap, which, vb, vkey, store_ap):
                tt_op("dve", ksc[:], ksrc_ap, kfb[:, which, :].unsqueeze(2).broadcast_to([128, 4, 128]), ALU.mult, ["qkr1", "qkr2", "kfb"], ["ksc"])
                (pa, pb2), (ka, kb2) = bank2()
                for h in range(4):
                    pp = pa if h < 2 else pb2; kk = ka if h < 2 else kb2
                    P.op("pe", lambda e, pp=pp, h=h: e.matmul(pp[:, (h % 2) * 256:(h % 2 + 1) * 256], lhsT=ksc[:, h, :], rhs=vb[:, h * 256:(h + 1) * 256], start=True, stop=True),
                         reads=["ksc", vkey], writes=[kk])
                if store_ap is not None:
                    P.op("act", lambda e: e.copy(out=store_ap, in_=Srun[:]), reads=["Srun"], writes=[("Sb_all", n)])
                for h in range(4):
                    pp = pa if h < 2 else pb2; kk = ka if h < 2 else kb2
                    P.op("dve", lambda e, pp=pp, h=h: e.scalar_tensor_tensor(out=Srun[:, h, :], in0=Srun[:, h, :], scalar=GAMMA_C[h], in1=pp[:, (h % 2) * 256:(h % 2 + 1) * 256], op0=ALU.mult, op1=ALU.add),
                         reads=[kk, "Srun"], writes=["Srun"])

            P.op("dve", lambda e: e.memset(Srun[:], 0.0), writes=["Srun"])
            for n in reversed(range(NT)):
                i = n % 2
                r0 = n * 128
                P.dma("sp", qk[i][:, 4:8, :], proj[r0:r0 + 128, 512:1024].rearrange("p (h d) -> p h d", h=4), reads=[("proj", n, 1)], writes=[("qk", i)])
                P.dma("sp", vv[i][:], proj[r0:r0 + 128, 1024:2048], reads=[("proj", n, 2), ("proj", n, 3)], writes=[("vv", i)])
                rope(qk[i][:, 4:8, :], 4, n, ("qk", i))
                kv_update(n, qkr[:, 0:4, :], 1, vv[i], ("vv", i), Sb_all[:, n])
                yield
            P.op("dve", lambda e: e.memset(Srun[:], 0.0), reads=[], writes=["Srun"])
            P.op("dve", lambda e: e.memset(Sbf[:], 0.0), writes=["Sbf"])
            for n in range(0 if os.environ.get('RPASS') == '1' else NT):
                i = n % 2
                r0 = n * 128
                P.dma("sp", qk[i][:], proj[r0:r0 + 128, 0:1024].rearrange("p (h d) -> p h d", h=8), reads=[("proj", n, 0), ("proj", n, 1)], writes=[("qk", i)])
                P.dma("sp", vv[i][:], proj[r0:r0 + 128, 1024:2048], reads=[("proj", n, 2), ("proj", n, 3)], writes=[("vv", i)])
                P.dma("sp", rg[i][:], proj[r0:r0 + 128, 2048:3072], reads=[("proj", n, 4), ("proj", n, 5)], writes=[("rg", i)])
                rope(qk[i][:], 8, n, ("qk", i))
                pt, pk = transposes(lambda b: qkr[:, b, :], 8, None, None, ["qkr1", "qkr2"])
                ptv = pt[:].rearrange("p (a t) -> p a t", a=8)
                P.op("act", lambda e, ptv=ptv: e.copy(out=qT[:], in_=ptv[:, 0:4, :]), reads=[pk], writes=["qT"])
                P.op("act", lambda e, ptv=ptv: e.copy(out=kT[:], in_=ptv[:, 4:8, :]), reads=[pk], writes=["kT"])
                tt_op("dve", qfT[:], ptv[:, 0:4, :], ftab[:], ALU.mult, [pk, "ftab"], ["qfT"])
                tt_op("dve", qbT[:], ptv[:, 0:4, :], btab[:], ALU.mult, [pk, "btab"], ["qbT"])
                ps, pks = bank()
                for h in range(4):
                    P.op("pe", lambda e, ps=ps, h=h: e.matmul(ps[:, h * 128:(h + 1) * 128], lhsT=kT[:, h, :], rhs=qT[:, h, :], start=True, stop=True), reads=["kT", "qT"], writes=[pks], inc=(h == 3))
                tt_op("dve", sT[:], ps[:].rearrange("p (h t) -> p h t", h=4), rmask[:], ALU.mult, [pks, "rmask"], ["sT"])
                (oa, ob), (koa, kob) = bank2()
                for h in range(4):
                    pp = oa if h < 2 else ob; kk = koa if h < 2 else kob
                    osl = pp[:, (h % 2) * 256:(h % 2 + 1) * 256]
                    P.op("pe", lambda e, osl=osl, h=h, i=i: e.matmul(osl, lhsT=sT[:, h, :], rhs=vv[i][:, h * 256:(h + 1) * 256], start=True, stop=False), reads=["sT", ("vv", i)], writes=[kk], inc=False)
                    P.op("pe", lambda e, osl=osl, h=h: e.matmul(osl, lhsT=qfT[:, h, :], rhs=Sbf[:, h, :], start=False, stop=False), reads=["qfT", "Sbf"], writes=[kk], inc=False)
                    P.op("pe", lambda e, osl=osl, h=h, n=n: e.matmul(osl, lhsT=qbT[:, h, :], rhs=Sb_all[:, n, h, :], start=False, stop=True), reads=["qbT", ("Sb_all", n)], writes=[kk], inc=True)
                kv_update(n, qkr[:, 4:8, :], 0, vv[i], ("vv", i), None)
                P.op("act", lambda e: e.copy(out=Sbf[:], in_=Srun[:]), reads=["Srun"], writes=["Sbf"])
                for hf, (pp, kk) in enumerate(((oa, koa), (ob, kob))):
                    P.op("dve", lambda e, pp=pp, hf=hf: e.tensor_reduce(out=st4[:, 0, hf * 2:hf * 2 + 2], in_=pp[:].rearrange("p (h e) -> p h e", h=2), axis=AX.X, op=ALU.add), reads=[kk], writes=["st_sum"])
                    P.op("act", lambda e, pp=pp, hf=hf: e.activation(out=sqs[:, hf * 512:(hf + 1) * 512], in_=pp[:], func=AF.Square), reads=[kk], writes=["sqs"])
                P.op("dve", lambda e: e.tensor_reduce(out=st4[:, 1, :], in_=sqs[:].rearrange("p (h e) -> p h e", h=4), axis=AX.X, op=ALU.add), reads=["sqs"], writes=["st_ssq"])
                P.op("dve", lambda e: e.tensor_scalar(out=st4[:, 2, :], in0=st4[:, 0, :], scalar1=1.0 / 256, scalar2=None, op0=ALU.mult), reads=["st_sum"], writes=["st_mean"])
                tt_op("dve", st4[:, 4, :], st4[:, 2, :], st4[:, 2, :], ALU.mult, ["st_mean"], ["st_msq"])
                P.op("dve", lambda e: e.scalar_tensor_tensor(out=st4[:, 3, :], in0=st4[:, 1, :], scalar=1.0 / 256, in1=st4[:, 4, :], op0=ALU.mult, op1=ALU.subtract), reads=["st_ssq", "st_msq"], writes=["st_var"])
                P.op("act", lambda e: e.activation(out=st4[:, 3, :], in_=st4[:, 3, :], func=AF.Sqrt, bias=epsc[:], scale=1.0), reads=["st_var", "epsc"], writes=["st_var"])
                P.op("dve", lambda e: e.reciprocal(out=st4[:, 3, :], in_=st4[:, 3, :]), reads=["st_var"], writes=["st_var"])
                P.op("dve", lambda e: e.scalar_tensor_tensor(out=st4[:, 5, :], in0=st4[:, 2, :], scalar=-1.0, in1=st4[:, 3, :], op0=ALU.mult, op1=ALU.mult), reads=["st_mean", "st_var"], writes=["st_nmr"])
                for h in range(4):
                    pp = oa if h < 2 else ob; kk = koa if h < 2 else kob
                    P.op("act", lambda e, pp=pp, h=h: e.activation(out=yn[:, h * 256:(h + 1) * 256], in_=pp[:, (h % 2) * 256:(h % 2 + 1) * 256], func=AF.Identity, scale=st4[:, 3, h:h + 1], bias=st4[:, 5, h:h + 1]),
                         reads=[kk, "st_var", "st_nmr"], writes=["yn"])
                P.op("act", lambda e, i=i: e.activation(out=sg[:], in_=rg[i][:], func=AF.Silu), reads=[("rg", i)], writes=["sg"])
                tt_op("dve", yg[:].rearrange("p a t -> p (a t)"), yn[:], sg[:], ALU.mult, ["yn", "sg"], ["yg"])
                pt2, pk2 = transposes(lambda b: yg[:, b, :], 8, None, None, ["yg"])
                ys = ystg[i]
                P.op("act", lambda e, pt2=pt2, ys=ys: e.copy(out=ys[:].rearrange("p a t -> p (a t)"), in_=pt2[:]), reads=[pk2], writes=[("ystg", i)])
                P.dma("sp", yT[0:8, :, r0:r0 + 128].rearrange("k p t -> p k t"), ys[:], reads=[("ystg", i)], writes=[("yT", 0, n)])
                yield
      yield

    def g_gen(es):
     if True:
      if "G" not in SKIP:
            tri = sbt(es, U("tri"), [128, 4, 128], F32)
            P.dma("sp", tri[:], cin["tri"][:], writes=["tri"])
            wgt = sbt(es, U("wgt"), [32, 512], F32)
            bgt = sbt(es, U("bgt"), [1, 512], F32)
            P.dma("sp", wgt[:], B.wgate[l], writes=["wgt"])
            P.dma("sp", bgt[:], B.bgate[l], writes=["bgt"])
            glr = [sbt(es, U("glr"), [32, 128], F32) for _ in range(2)]
            gqk = [sbt(es, U("gqk"), [128, 512], BF16) for _ in range(2)]
            gv = [sbt(es, U("gv"), [128, 512], BF16) for _ in range(2)]
            gr = [sbt(es, U("gr"), [128, 512], BF16) for _ in range(2)]
            e1 = sbt(es, U("e1"), [128, 512], F32)
            ll = sbt(es, U("ll"), [128, 512], F32)
            Eq = sbt(es, U("Eq"), [128, 512], F32)
            Ek = sbt(es, U("Ek"), [128, 512], F32)
            Est = sbt(es, U("Est"), [128, 256], F32)
            qin = sbt(es, U("qin"), [128, 2, 256], BF16)
            kin = sbt(es, U("kin"), [128, 2, 256], BF16)
            kst = sbt(es, U("kst"), [128, 256], BF16)
            qTm = sbt(es, U("qTm"), [128, 2, 4, 128], BF16)
            kTg = sbt(es, U("kTg"), [128, 4, 128], BF16)
            tm1 = sbt(es, U("tm1"), [128, 4, 128], F32)
            tm2 = sbt(es, U("tm2"), [128, 4, 128], F32)
            sTg = sbt(es, U("sTg"), [128, 4, 128], BF16)
            Sg_all = sbt(es, U("Sg_all"), [128, NT, 2, 128], BF16)
            Sgrun = sbt(es, U("Sgrun"), [128, 2, 128], F32)
            Sgbf = sbt(es, U("Sgbf"), [128, 2, 128], BF16)
            dcol = sbt(es, U("dcol"), [128, 2], F32)
            gsq = sbt(es, U("gsq"), [128, 512], F32)
            gst = sbt(es, U("gst"), [128, 2, 4], F32)
            gyn = sbt(es, U("gyn"), [128, 512], F32)
            gsg = sbt(es, U("gsg"), [128, 512], F32)
            gyg = sbt(es, U("gyg"), [128, 4, 128], BF16)
            gys = [sbt(es, U("gys"), [128, 4, 128], BF16) for _ in range(2)]

            def gates(n, i):
                c0 = n * 128
                P.dma("sp", glr[i][:], glrT[:, c0:c0 + 128], reads=[("glrT", c0 // 512)], writes=[("glr", i)])
                px, pkx = bank()
                P.op("pe", lambda e, px=px, i=i: e.matmul(px[:], lhsT=glr[i][:], rhs=wgt[:], start=True, stop=False), reads=[("glr", i), "wgt"], writes=[pkx], inc=False)
                P.op("pe", lambda e, px=px: e.matmul(px[:], lhsT=onesf[0:1, :], rhs=bgt[:], start=False, stop=True), reads=["onesf", "bgt"], writes=[pkx])
                P.op("act", lambda e, px=px: e.activation(out=e1[:], in_=px[:], func=AF.Exp, scale=-1.0), reads=[pkx], writes=["e1"])
                P.op("act", lambda e: e.activation(out=ll[:], in_=e1[:], func=AF.Ln, bias=1.0), reads=["e1"], writes=["ll"])

            def state_update(n, d, i, store_ap):
                ptot, pkt = bank()
                for pr in range(2):
                    P.op("pe", lambda e, ptot=ptot, pr=pr, d=d: e.matmul(ptot[:, pr:pr + 1], lhsT=ll[:, d * 256 + pr * 128:d * 256 + (pr + 1) * 128], rhs=onesf[:, 0:1], start=True, stop=True),
                         reads=["ll", "onesf"], writes=[pkt], inc=(pr == 1))
                P.op("act", lambda e, ptot=ptot: e.activation(out=dcol[:], in_=ptot[:, 0:2], func=AF.Exp, scale=-1.0 / 16), reads=[pkt], writes=["dcol"])
                pkv, pkk = bank()
                for pr in range(2):
                    P.op("pe", lambda e, pkv=pkv, pr=pr, i=i: e.matmul(pkv[:, pr * 256:(pr + 1) * 256], lhsT=kst[:, pr * 128:(pr + 1) * 128], rhs=gv[i][:, pr * 256:(pr + 1) * 256], start=True, stop=True),
                         reads=["kst", ("gv", i)], writes=[pkk], inc=(pr == 1))
                if store_ap is not None:
                    P.op("act", lambda e: e.copy(out=store_ap, in_=Sgrun[:]), reads=["Sgrun"], writes=[("Sg_all", n)])
                for pr in range(2):
                    for hh in range(2):
                        rs = slice(hh * 64, (hh + 1) * 64)
                        P.op("dve", lambda e, pkv=pkv, pr=pr, hh=hh, rs=rs: e.scalar_tensor_tensor(out=Sgrun[rs, pr, :], in0=Sgrun[rs, pr, :], scalar=dcol[rs, pr:pr + 1], in1=pkv[rs, pr * 256 + hh * 128:pr * 256 + (hh + 1) * 128], op0=ALU.mult, op1=ALU.add),
                             reads=[pkk, "Sgrun", "dcol"], writes=["Sgrun"])

            P.op("dve", lambda e: e.memset(Sgrun[:], 0.0), writes=["Sgrun"])
            for n in reversed(range(NT)):
                i = n % 2; r0 = n * 128
                P.dma("sp", gqk[i][:], proj[r0:r0 + 128, 3584:4096], reads=[("proj", n, 7)], writes=[("gqk", i)])
                P.dma("sp", gv[i][:], proj[r0:r0 + 128, 4096:4608], reads=[("proj", n, 8)], writes=[("gv", i)])
                gates(n, i)
                pr_, pkr = bank()
                P.op("pe", lambda e, pr_=pr_: e.matmul(pr_[:, 0:256], lhsT=tri[:, 3, :], rhs=ll[:, 256:512], start=True, stop=True), reads=["tri", "ll"], writes=[pkr])
                P.op("act", lambda e, pr_=pr_: e.activation(out=Est[:], in_=pr_[:, 0:256], func=AF.Exp, scale=-1.0 / 16), reads=[pkr], writes=["Est"])
                tt_op("dve", kst[:], gqk[i][:, 256:512], Est[:], ALU.mult, [("gqk", i), "Est"], ["kst"])
                state_update(n, 1, i, Sg_all[:, n])
                yield
            P.op("dve", lambda e: e.memset(Sgrun[:], 0.0), writes=["Sgrun"])
            P.op("dve", lambda e: e.memset(Sgbf[:], 0.0), writes=["Sgbf"])
            for n in range(NT):
                i = n % 2; r0 = n * 128
                P.dma("sp", gqk[i][:], proj[r0:r0 + 128, 3584:4096], reads=[("proj", n, 7)], writes=[("gqk", i)])
                P.dma("sp", gv[i][:], proj[r0:r0 + 128, 4096:4608], reads=[("proj", n, 8)], writes=[("gv", i)])
                P.dma("sp", gr[i][:], proj[r0:r0 + 128, 4608:5120], reads=[("proj", n, 9)], writes=[("gr", i)])
                gates(n, i)
                (pa, pb2), (ka, kb2) = bank2()
                P.op("pe", lambda e, pa=pa: e.matmul(pa[:, 0:256], lhsT=tri[:, 0, :], rhs=ll[:, 0:256], start=True, stop=True), reads=["tri", "ll"], writes=[ka], inc=False)
                P.op("pe", lambda e, pa=pa: e.matmul(pa[:, 256:512], lhsT=tri[:, 2, :], rhs=ll[:, 256:512], start=True, stop=True), reads=["tri", "ll"], writes=[ka])
                P.op("pe", lambda e, pb2=pb2: e.matmul(pb2[:, 0:256], lhsT=tri[:, 1, :], rhs=ll[:, 0:256], start=True, stop=True), reads=["tri", "ll"], writes=[kb2])
                P.op("act", lambda e, pa=pa: e.activation(out=Eq[:], in_=pa[:], func=AF.Exp, scale=-1.0 / 16), reads=[ka], writes=["Eq"])
                P.op("act", lambda e, pa=pa: e.activation(out=Ek[:], in_=pa[:], func=AF.Exp, scale=1.0 / 16), reads=[ka], writes=["Ek"])
                P.op("act", lambda e, pb2=pb2: e.activation(out=Est[:], in_=pb2[:, 0:256], func=AF.Exp, scale=-1.0 / 16), reads=[kb2], writes=["Est"])
                gq_b = gqk[i][:, 0:256].unsqueeze(1).broadcast_to([128, 2, 256])
                gk_b = gqk[i][:, 256:512].unsqueeze(1).broadcast_to([128, 2, 256])
                P.op("dve", lambda e, gq_b=gq_b: e.scalar_tensor_tensor(out=qin[:], in0=Eq[:].rearrange("p (d c) -> p d c", d=2), scalar=0.125, in1=gq_b, op0=ALU.mult, op1=ALU.mult), reads=["Eq", ("gqk", i)], writes=["qin"])
                tt_op("dve", kin[:], Ek[:].rearrange("p (d c) -> p d c", d=2), gk_b, ALU.mult, ["Ek", ("gqk", i)], ["kin"])
                tt_op("dve", kst[:], gqk[i][:, 256:512], Est[:], ALU.mult, [("gqk", i), "Est"], ["kst"])
                def blk(b):
                    src = qin if b < 4 else kin
                    bb = b % 4
                    return src[:, bb // 2, (bb % 2) * 128:(bb % 2 + 1) * 128]
                pt, pk = transposes(blk, 8, None, None, ["qin", "kin"])
                ptv = pt[:].rearrange("p (a t) -> p a t", a=8)
                P.op("act", lambda e, ptv=ptv: e.activation(out=qTm[:, 0], in_=ptv[:, 0:4, :], func=AF.Identity, scale=tri[:, 3, 64:65]), reads=[pk, "tri"], writes=["qTm0"])
                P.op("act", lambda e, ptv=ptv: e.activation(out=qTm[:, 1], in_=ptv[:, 0:4, :], func=AF.Identity, scale=tri[:, 2, 64:65]), reads=[pk, "tri"], writes=["qTm1"])
                P.op("dve", lambda e, ptv=ptv: e.tensor_copy(out=kTg[:], in_=ptv[:, 4:8, :]), reads=[pk], writes=["kTg"])
                (sa, sb2), (ksa, ksb) = bank2()
                for d in range(2):
                    pp = sa if d == 0 else sb2; kk = ksa if d == 0 else ksb
                    for h in range(4):
                        pr = h // 2; hh = h % 2
                        P.op("pe", lambda e, pp=pp, h=h, d=d, pr=pr, hh=hh: e.matmul(pp[:, h * 128:(h + 1) * 128], lhsT=kTg[:, d * 2 + pr, :], rhs=qTm[:, hh, d * 2 + pr, :], start=True, stop=True),
                             reads=["kTg", "qTm0", "qTm1"], writes=[kk], inc=(h == 3))
                tt_op("dve", tm1[:], sa[:].rearrange("p (h t) -> p h t", h=4), tri[:, 0, :].unsqueeze(1).broadcast_to([128, 4, 128]), ALU.mult, [ksa, "tri"], ["tm1"])
                tt_op("dve", tm2[:], sb2[:].rearrange("p (h t) -> p h t", h=4), tri[:, 1, :].unsqueeze(1).broadcast_to([128, 4, 128]), ALU.mult, [ksb, "tri"], ["tm2"])
                tt_op("dve", sTg[:], tm1[:], tm2[:], ALU.add, ["tm1", "tm2"], ["sTg"])
                po, pko = bank()
                for h in range(4):
                    pr = h // 2; hh = h % 2
                    osl = po[:, h * 128:(h + 1) * 128]
                    P.op("pe", lambda e, osl=osl, h=h, i=i: e.matmul(osl, lhsT=sTg[:, h, :], rhs=gv[i][:, h * 128:(h + 1) * 128], start=True, stop=False), reads=["sTg", ("gv", i)], writes=[pko], inc=False)
                    P.op("pe", lambda e, osl=osl, pr=pr, hh=hh: e.matmul(osl, lhsT=qTm[:, hh, pr, :], rhs=Sgbf[:, pr, :], start=False, stop=False), reads=["qTm0", "qTm1", "Sgbf"], writes=[pko], inc=False)
                    P.op("pe", lambda e, osl=osl, pr=pr, hh=hh, n=n: e.matmul(osl, lhsT=qTm[:, hh, 2 + pr, :], rhs=Sg_all[:, n, pr, :], start=False, stop=True), reads=["qTm0", "qTm1", ("Sg_all", n)], writes=[pko], inc=True)
                state_update(n, 0, i, None)
                P.op("act", lambda e: e.copy(out=Sgbf[:], in_=Sgrun[:]), reads=["Sgrun"], writes=["Sgbf"])
                P.op("act", lambda e, po=po: e.activation(out=gsq[:], in_=po[:], func=AF.Square), reads=[pko], writes=["gsq"])
                P.op("dve", lambda e: e.tensor_reduce(out=gst[:, 0, :], in_=gsq[:].rearrange("p (h e) -> p h e", h=4), axis=AX.X, op=ALU.add), reads=["gsq"], writes=["gst0"])
                P.op("act", lambda e: e.activation(out=gst[:, 1, :], in_=gst[:, 0, :], func=AF.Sqrt, bias=epsc[:], scale=1.0 / 128), reads=["gst0", "epsc"], writes=["gst1"])
                P.op("dve", lambda e: e.reciprocal(out=gst[:, 1, :], in_=gst[:, 1, :]), reads=["gst1"], writes=["gst1"])
                tt_op("dve", gyn[:].rearrange("p (h e) -> p h e", h=4), po[:].rearrange("p (h e) -> p h e", h=4), gst[:, 1, :].unsqueeze(2).broadcast_to([128, 4, 128]), ALU.mult, [pko, "gst1"], ["gyn"])
                P.op("act", lambda e, i=i: e.activation(out=gsg[:], in_=gr[i][:], func=AF.Silu), reads=[("gr", i)], writes=["gsg"])
                tt_op("dve", gyg[:].rearrange("p a t -> p (a t)"), gyn[:], gsg[:], ALU.mult, ["gyn", "gsg"], ["gyg"])
                pt2, pk2 = transposes(lambda b: gyg[:, b, :], 4, None, None, ["gyg"])
                ys = gys[i]
                P.op("act", lambda e, pt2=pt2, ys=ys: e.copy(out=ys[:].rearrange("p a t -> p (a t)"), in_=pt2[:, 0:512]), reads=[pk2], writes=[("gys", i)])
                P.dma("sp", yT[12:16, :, r0:r0 + 128].rearrange("k p t -> p k t"), ys[:], reads=[("gys", i)], writes=[("yT", 2, n)])
                yield
      yield

    with contextlib.ExitStack() as es_rg:
        gens = [r_gen(es_rg), g_gen(es_rg)]
        if os.environ.get("MIX_NOINTER") == "1":
            for g_ in gens:
                for _ in g_:
                    pass
                P.barrier()
        else:
            while gens:
                for g_ in list(gens):
                    try:
                        next(g_)
                    except StopIteration:
                        gens.remove(g_)
        P.barrier()

    if "S5" not in SKIP:
        s5_phase(B, l)
    P.barrier()

    with contextlib.ExitStack() as es:
        wo = sbt(es, U("wo"), [128, KT, D], BF16)
        for q in range(4):
            P.dma("pool", wo[:, :, q * 512:(q + 1) * 512], B.w_out[l, :, q * 512:(q + 1) * 512].rearrange("(k p) n -> p k n", p=128), writes=[("wo", q)])
        ych = [sbt(es, U("ych"), [128, KT, 512], BF16) for _ in range(2)]
        hup = [sbt(es, U("ohup"), [128, 512], F32) for _ in range(3)]
        hi = 0
        for c in range(NC5):
            i = c % 2
            t0 = c * 512
            P.dma("sp", ych[i][:], yT[:, :, t0:t0 + 512].rearrange("k p t -> p k t"), reads=[("yT", a, n) for a in range(3) for n in range(c * 4, c * 4 + 4)], writes=[("ych", i)])
            for f in range(KT):
                hu = hup[hi % 3]; hk = ("ohup", hi % 3); hi += 1
                hkeys = [("hT", f, n) for n in range(c * 4, c * 4 + 4)]
                P.dma("sp", hu[:], hT[f, :, t0:t0 + 512], reads=hkeys, writes=[hk])
                pb, pk = bank()
                for kt in range(KT):
                    P.op("pe", lambda e, pb=pb, kt=kt, f=f, i=i: e.matmul(pb[:], lhsT=wo[:, kt, f * 128:(f + 1) * 128], rhs=ych[i][:, kt, :], start=(kt == 0), stop=(kt == KT - 1)),
                         reads=[("wo", f // 4), ("ych", i)], writes=[pk], inc=(kt == KT - 1))
                P.op("dve", lambda e, pb=pb, hu=hu, f=f: e.scalar_tensor_tensor(out=hu[:], in0=pb[:], scalar=g05[:, 1, f:f + 1], in1=hu[:], op0=ALU.mult, op1=ALU.add), reads=[pk, hk, "g05"], writes=[hk])
                P.dma("sp", hT[f, :, t0:t0 + 512], hu[:], reads=[hk], writes=hkeys)
    P.barrier()


F32 = mybir.dt.float32
BF16 = mybir.dt.bfloat16
AF = mybir.ActivationFunctionType
ALU = mybir.AluOpType
AX = mybir.AxisListType

D = 2048
KT = 16
DFF = 5632
NJ = 44
DIN = 5152
NMOD = 9
EPS = 1e-6
LOG_GAMMA = [math.log1p(-2.0 ** (-5.0 - h)) for h in range(4)]


def host_consts(T):
    c = {}
    NT = T // 128
    pos = np.arange(T, dtype=np.float32)
    inv_freq = (10000.0 ** (-np.arange(0, 128, 2, dtype=np.float32) / 128)).astype(np.float32)
    ang = pos[:, None] * inv_freq[None, :]
    cos = np.cos(ang).astype(np.float32).reshape(NT, 128, 64).transpose(1, 0, 2)
    sin = np.sin(ang).astype(np.float32).reshape(NT, 128, 64).transpose(1, 0, 2)
    c["ropec"] = np.ascontiguousarray(cos)
    c["ropes"] = np.ascontiguousarray(sin)
    idx = np.arange(128, dtype=np.float64)
    lg = np.array(LOG_GAMMA, dtype=np.float64)
    dist = np.abs(idx[:, None] - idx[None, :])
    rmask = np.exp(dist[:, None, :] * lg[None, :, None]) * (128 ** -0.5)
    c["rmask"] = rmask.astype(np.float32)
    ftab = np.exp((idx + 1.0)[None, :] * lg[:, None]) * (128 ** -0.5)
    btab = np.exp((128 - idx)[None, :] * lg[:, None]) * (128 ** -0.5)
    c["ftab"] = np.broadcast_to(ftab[None], (128, 4, 128)).astype(np.float32).copy()
    c["btab"] = np.broadcast_to(btab[None], (128, 4, 128)).astype(np.float32).copy()
    kf = np.exp((127.0 - idx)[:, None] * lg[None, :])
    kb = np.exp(idx[:, None] * lg[None, :])
    c["kfb"] = np.stack([kf, kb], axis=1).astype(np.float32)
    j = idx[:, None]; i = idx[None, :]
    tri = np.stack([(j <= i), (j > i), (j >= i), (j < i)], axis=1).astype(np.float32)
    c["tri"] = tri
    c["iota64"] = np.broadcast_to(np.arange(64, dtype=np.float32)[None], (128, 64)).copy()
    c["ident"] = np.eye(128, dtype=np.float32)
    c["ones"] = np.ones((128, 128), dtype=np.float32)
    return c


CONST_SHAPES = lambda T: {k: v.shape for k, v in host_consts(128 * 1).items()}


class B:
    pass


def build(T, depth, TT=None, HQ=22, dbg=False):
    NT = T // 128
    if TT is None:
        TT = min(T, 1024)
    NTT = T // TT
    NCH = TT // 512
    nc = bass.Bass("TRN2", target_bir_lowering=False)
    P = Prog(nc)
    dr = {}

    def din(name, shape, dt=F32):
        dr[name] = nc.dram_tensor(name, list(shape), dt, kind="ExternalInput").ap()
        return dr[name]

    def dscr(name, shape, dt=F32):
        dr[name] = nc.dram_tensor(name, list(shape), dt, kind="Internal").ap()
        return dr[name]

    L = depth
    x = din("x", [T, D])
    cT = din("cT", [128, KT])
    w_ada = din("w_ada", [L, D, NMOD * D])
    b_ada = din("b_ada", [L, NMOD * D])
    gcols = din("gcols", [L, 3, 128, KT])
    gfin = din("gfin", [128, KT])
    fw = {}
    for nm in ("ffn1_w1", "ffn1_w3", "ffn2_w1", "ffn2_w3"):
        fw[nm] = din(nm, [L, D, DFF])
    for nm in ("ffn1_w2", "ffn2_w2"):
        fw[nm] = din(nm, [L, DFF, D])
    w_in = din("w_in", [L, D, DIN])
    w_out = din("w_out", [L, D, D])
    s5p = din("s5p", [L, 2, 128, 3, 16])
    s5B = din("s5B", [L, 2, 2, 16, 128, 128])
    s5C = din("s5C", [L, 2, 2, 16, 128, 128])
    s5d = din("s5d", [L, 128, 4])
    wglu = din("wglu", [L, 512, 512])
    bglu = din("bglu", [L, 128, 4])
    wgate = din("wgate", [L, 32, 512])
    bgate = din("bgate", [L, 1, 512])
    cshape = {k: v.shape for k, v in host_consts(T).items()}
    cin = {k: din("c_" + k, list(s)) for k, s in cshape.items()}
    out = nc.dram_tensor("out", [T, D], F32, kind="ExternalOutput").ap()
    dr["out"] = out

    hT = dscr("hT", [KT, 128, T])
    modrow = dscr("modrow", [L, NMOD * D])
    proj = dscr("proj", [T, 5120], BF16)
    suT = dscr("suT", [4, 128, T], BF16)
    glrT = dscr("glrT", [32, T])
    yT = dscr("yT", [KT, 128, T], BF16)
    taps = {}
    if dbg:
        taps["yT"] = nc.dram_tensor("tap_yT", [KT, 128, T], BF16, kind="ExternalOutput").ap()
        taps["hT"] = nc.dram_tensor("tap_hT", [KT, 128, T], F32, kind="ExternalOutput").ap()

    uid = [0]

    def U(s):
        uid[0] += 1
        return "%s_%d" % (s, uid[0])

    with contextlib.ExitStack() as top:
        pbf = [top.enter_context(nc.psum_tensor("pb%d" % i, [128, 512], F32)) for i in range(6)]
        pbb = [top.enter_context(nc.psum_tensor("pt%d" % i, [128, 1024], BF16)) for i in range(2)]
        rr = {"f": 0, "b": 0}

        def bank():
            i = rr["f"]; rr["f"] = (i + 1) % 6
            return pbf[i], ("pb", i)

        def bank2():
            i = rr["f"]
            if i % 2:
                i = (i + 1) % 6
            rr["f"] = (i + 2) % 6
            return (pbf[i], pbf[i + 1]), (("pb", i), ("pb", i + 1))

        def bankb():
            i = rr["b"]; rr["b"] = (i + 1) % 2
            return pbb[i], ("pt", i)

        qrr = [0]

        def dq():
            qrr[0] ^= 1
            return "sp" if qrr[0] else "act"

        sbt = lambda es, name, shape, dt: es.enter_context(nc.sbuf_tensor(name, list(shape), dt))
        ident = sbt(top, "ident", [128, 128], BF16)
        onesb = sbt(top, "onesb", [128, 128], BF16)
        onesf = sbt(top, "onesf", [128, 128], F32)
        modc = sbt(top, "modc", [128, NMOD * KT], F32)
        gcol = sbt(top, "gcol", [128, 3, KT], F32)
        acol = sbt(top, "acol", [128, 3, KT], F32)
        g05 = sbt(top, "g05", [128, 3, KT], F32)
        condT = sbt(top, "condT", [128, KT], BF16)
        epsc = sbt(top, "epsc", [128, 1], F32)
        pic = sbt(top, "pic", [128, 1], F32)

        with contextlib.ExitStack() as es:
            tmpf = sbt(es, "tmpf", [128, 128], F32)
            P.dma("sp", tmpf[:], cin["ident"][:], writes=["tmpf"])
            P.op("dve", lambda e: e.tensor_copy(out=ident[:], in_=tmpf[:]), reads=["tmpf"], writes=["ident"])
            P.dma("sp", onesf[:], cin["ones"][:], writes=["onesf"])
            P.op("dve", lambda e: e.tensor_copy(out=onesb[:], in_=onesf[:]), reads=["onesf"], writes=["onesb"])
            P.op("dve", lambda e: e.memset(epsc[:], EPS), writes=["epsc"])
            P.op("dve", lambda e: e.memset(pic[:], math.pi), writes=["pic"])
            cTf = sbt(es, "cTf", [128, KT], F32)
            P.dma("sp", cTf[:], cT[:], writes=["cTf"])
            P.op("act", lambda e: e.activation(out=condT[:], in_=cTf[:], func=AF.Silu), reads=["cTf"], writes=["condT"])

            identf = sbt(es, "identf", [128, 128], F32)
            P.dma("sp", identf[:], cin["ident"][:], writes=["identf"])
            xin = [sbt(es, "xin%d" % i, [128, D], F32) for i in range(2)]
            hst = [sbt(es, "hst%d" % i, [128, KT, 128], F32) for i in range(2)]
            for n in range(NT):
                xb_ = xin[n % 2]; hb_ = hst[n % 2]
                P.dma("sp", xb_[:], x[n * 128:(n + 1) * 128, :], writes=[("xin", n % 2)])
                for g4 in range(4):
                    pb, pk = bank()
                    for q in range(4):
                        kt = g4 * 4 + q
                        P.op("pe", lambda e, pb=pb, q=q, kt=kt, xb_=xb_: e.transpose(out=pb[:, q * 128:(q + 1) * 128], in_=xb_[:, kt * 128:(kt + 1) * 128], identity=identf[:]),
                             reads=[("xin", n % 2), "identf"], writes=[pk], inc=(q == 3))
                    eng = "act" if g4 % 2 else "dve"
                    if eng == "act":
                        P.op("act", lambda e, pb=pb, hb_=hb_, g4=g4: e.copy(out=hb_[:, g4 * 4:(g4 + 1) * 4, :], in_=pb[:].rearrange("p (a b) -> p a b", a=4)), reads=[pk], writes=[("hst", n % 2, g4)])
                    else:
                        P.op("dve", lambda e, pb=pb, hb_=hb_, g4=g4: e.tensor_copy(out=hb_[:, g4 * 4:(g4 + 1) * 4, :], in_=pb[:].rearrange("p (a b) -> p a b", a=4)), reads=[pk], writes=[("hst", n % 2, g4)])
                P.dma("sp", hT[:, :, n * 128:(n + 1) * 128].rearrange("k p t -> p k t"), hb_[:], reads=[("hst", n % 2, g) for g in range(4)], writes=[("hT", k, n) for k in range(KT)])

            wab = [sbt(es, "wab%d" % i, [128, KT, 512], BF16) for i in range(2)]
            brow = [sbt(es, "brow%d" % i, [1, 512], F32) for i in range(2)]
            mrow = [sbt(es, "mrow%d" % i, [1, 512], F32) for i in range(2)]
            it = 0
            for l in range(L):
                for ch in range(NMOD * D // 512):
                    wb_ = wab[it % 2]; br_ = brow[it % 2]; mr_ = mrow[it % 2]
                    P.dma("pool", wb_[:], w_ada[l, :, ch * 512:(ch + 1) * 512].rearrange("(k p) n -> p k n", p=128), writes=[("wab", it % 2)])
                    P.dma("sp", br_[:], b_ada[l:l + 1, ch * 512:(ch + 1) * 512], writes=[("brow", it % 2)])
                    pb, pk = bank()
                    for kt in range(KT):
                        P.op("pe", lambda e, pb=pb, kt=kt, wb_=wb_: e.matmul(pb[0:1, :], lhsT=condT[:, kt:kt + 1], rhs=wb_[:, kt, :], start=(kt == 0), stop=(kt == KT - 1)),
                             reads=[("wab", it % 2), "condT"], writes=[pk], inc=(kt == KT - 1))
                    P.op("dve", lambda e, pb=pb, br_=br_, mr_=mr_: e.tensor_tensor(out=mr_[:], in0=pb[0:1, :], in1=br_[:], op=ALU.add), reads=[pk, ("brow", it % 2)], writes=[("mrow", it % 2)])
                    P.dma("sp", modrow[l:l + 1, ch * 512:(ch + 1) * 512], mr_[:], reads=[("mrow", it % 2)], writes=[("modrow", l)])
                    it += 1
        P.barrier()

        def load_layer_cols(l):
            P.dma("sp", modc[:], modrow[l, :].rearrange("(v p) -> p v", p=128), reads=[("modrow", l)], writes=["modc"], allow_slow_non_contiguous=True)
            P.dma("sp", gcol[:], gcols[l].rearrange("s p k -> p s k"), writes=["gcol"])
            for s in range(3):
                P.op("dve", lambda e, s=s: e.scalar_tensor_tensor(out=acol[:, s, :], in0=modc[:, (3 * s + 1) * KT:(3 * s + 2) * KT], scalar=1.0, in1=gcol[:, s, :], op0=ALU.add, op1=ALU.mult),
                     reads=["modc", "gcol"], writes=["acol"])
                P.op("dve", lambda e, s=s: e.tensor_scalar(out=g05[:, s, :], in0=modc[:, (3 * s + 2) * KT:(3 * s + 3) * KT], scalar1=(1.0 if s == 1 else 0.5), scalar2=None, op0=ALU.mult),
                     reads=["modc"], writes=["g05"])

        def norm_mod(es_bufs, s, t0, ntok, uT, ukey, a_ap=None, b_ap=None):
            hch, sq, rstd = es_bufs
            for c in range(ntok // 512):
                tt0 = t0 + c * 512
                hk = U("hch")
                P.dma("sp", hch[:], hT[:, :, tt0:tt0 + 512].rearrange("k p t -> p k t"), reads=[("hT", k, n) for k in range(KT) for n in range(tt0 // 128, tt0 // 128 + 4)], writes=["hch"])
                P.op("act", lambda e: e.activation(out=sq[:], in_=hch[:], func=AF.Square), reads=["hch"], writes=["sq"])
                pb, pk = bank()
                for kt in range(KT):
                    P.op("pe", lambda e, pb=pb, kt=kt: e.matmul(pb[:], lhsT=onesb[:], rhs=sq[:, kt, :], start=(kt == 0), stop=(kt == KT - 1)), reads=["sq", "onesb"], writes=[pk], inc=(kt == KT - 1))
                P.op("act", lambda e, pb=pb: e.activation(out=rstd[:], in_=pb[:], func=AF.Sqrt, bias=epsc[:], scale=1.0 / D), reads=[pk, "epsc"], writes=["rstd"])
                P.op("dve", lambda e: e.reciprocal(out=rstd[:], in_=rstd[:]), reads=["rstd"], writes=["rstd"])
                P.op("dve", lambda e: e.tensor_tensor(out=hch[:], in0=hch[:], in1=rstd[:].unsqueeze(1).broadcast_to([128, KT, 512]), op=ALU.mult), reads=["hch", "rstd"], writes=["hch"])
                for kt in range(KT):
                    if a_ap is None:
                        sc_ap = acol[:, s, kt:kt + 1]
                        bi_ap = modc[:, (3 * s) * KT + kt:(3 * s) * KT + kt + 1]
                        rd = ["acol", "modc"]
                    else:
                        sc_ap = a_ap[:, kt:kt + 1]; bi_ap = 0.0; rd = ["gfinc"]
                    P.op("act", lambda e, kt=kt, c=c, sc_ap=sc_ap, bi_ap=bi_ap: e.activation(out=uT[:, kt, c * 512:(c + 1) * 512], in_=hch[:, kt, :], func=AF.Identity, scale=sc_ap, bias=bi_ap),
                         reads=["hch"] + rd, writes=[ukey])

        def ffn(l, s, w1, w3, w2):
            with contextlib.ExitStack() as es:
                uTs = [sbt(es, U("uT"), [128, KT, TT], BF16) for _ in range(2)]
                gT = sbt(es, U("gT"), [128, HQ, TT], BF16)
                hch = sbt(es, U("hch"), [128, KT, 512], F32)
                sq = sbt(es, U("sq"), [128, KT, 512], BF16)
                rstd = sbt(es, U("rstd"), [128, 512], F32)
                w13 = [sbt(es, U("w13"), [128, 2, KT, 128], BF16) for _ in range(3)]
                w2b = [sbt(es, U("w2b"), [128, HQ, 128], BF16) for _ in range(2)]
                sil = [sbt(es, U("sil"), [128, 512], F32) for _ in range(2)]
                hup = [sbt(es, U("hup"), [128, 512], F32) for _ in range(3)]
                wi = 0; w2i = 0; si = 0; hi = 0
                norm_mod((hch, sq, rstd), s, 0, TT, uTs[0], ("uT", 0))
                for tt in range(NTT):
                    t0 = tt * TT
                    uT = uTs[tt % 2]; ukey = ("uT", tt % 2)
                    for hp in range(NJ // HQ):
                        for jj in range(HQ):
                            j = hp * HQ + jj
                            wb_ = w13[wi % 3]; wk = ("w13", wi % 3); wi += 1
                            P.dma("pool", wb_[:, 0], w1[l, :, j * 128:(j + 1) * 128].rearrange("(k p) n -> p k n", p=128), writes=[wk])
                            P.dma("pool", wb_[:, 1], w3[l, :, j * 128:(j + 1) * 128].rearrange("(k p) n -> p k n", p=128), writes=[wk])
                            for c in range(NCH):
                                pa, pka = bank(); pb_, pkb = bank()
                                for kt in range(KT):
                                    P.op("pe", lambda e, pa=pa, kt=kt, c=c, wb_=wb_, uT=uT: e.matmul(pa[:], lhsT=wb_[:, 0, kt, :], rhs=uT[:, kt, c * 512:(c + 1) * 512], start=(kt == 0), stop=(kt == KT - 1)),
                                         reads=[wk, ukey], writes=[pka], inc=(kt == KT - 1))
                                for kt in range(KT):
                                    P.op("pe", lambda e, pb_=pb_, kt=kt, c=c, wb_=wb_, uT=uT: e.matmul(pb_[:], lhsT=wb_[:, 1, kt, :], rhs=uT[:, kt, c * 512:(c + 1) * 512], start=(kt == 0), stop=(kt == KT - 1)),
                                         reads=[wk, ukey], writes=[pkb], inc=(kt == KT - 1))
                                sl = sil[si % 2]; sk = ("sil", si % 2); si += 1
                                P.op("act", lambda e, pa=pa, sl=sl: e.activation(out=sl[:], in_=pa[:], func=AF.Silu), reads=[pka], writes=[sk])
                                P.op("dve", lambda e, pb_=pb_, sl=sl, jj=jj, c=c: e.tensor_tensor(out=gT[:, jj, c * 512:(c + 1) * 512], in0=pb_[:], in1=sl[:], op=ALU.mult), reads=[pkb, sk], writes=[("gT", jj)])
                        if hp == 0 and tt + 1 < NTT:
                            norm_mod((hch, sq, rstd), s, (tt + 1) * TT, TT, uTs[(tt + 1) % 2], ("uT", (tt + 1) % 2))
                        for f in range(KT):
                            w2_ = w2b[w2i % 2]; w2k = ("w2b", w2i % 2); w2i += 1
                            P.dma("pool", w2_[:], w2[l, hp * HQ * 128:(hp + 1) * HQ * 128, f * 128:(f + 1) * 128].rearrange("(j p) n -> p j n", p=128), writes=[w2k])
                            for c in range(NCH):
                                tt0 = t0 + c * 512
                                hu = hup[hi % 3]; hk = ("hup", hi % 3); hi += 1
                                hkeys = [("hT", f, n) for n in range(tt0 // 128, tt0 // 128 + 4)]
                                P.dma("sp", hu[:], hT[f, :, tt0:tt0 + 512], reads=hkeys, writes=[hk])
                                pb, pk = bank()
                                for jj in range(HQ):
                                    P.op("pe", lambda e, pb=pb, jj=jj, c=c, w2_=w2_: e.matmul(pb[:], lhsT=w2_[:, jj, :], rhs=gT[:, jj, c * 512:(c + 1) * 512], start=(jj == 0), stop=(jj == HQ - 1)),
                                         reads=[w2k, ("gT", jj)], writes=[pk], inc=(jj == HQ - 1))
                                P.op("dve", lambda e, pb=pb, hu=hu, f=f: e.scalar_tensor_tensor(out=hu[:], in0=pb[:], scalar=g05[:, s, f:f + 1], in1=hu[:], op0=ALU.mult, op1=ALU.add),
                                     reads=[pk, hk, "g05"], writes=[hk])
                                P.dma("sp", hT[f, :, tt0:tt0 + 512], hu[:], reads=[hk], writes=hkeys)
            P.barrier()

        B_ = B()
        B_.__dict__.update(locals())
        for l in range(L):
            load_layer_cols(l)
            ffn(l, 0, fw["ffn1_w1"], fw["ffn1_w3"], fw["ffn1_w2"])
            mixer(B_, l)
            ffn(l, 2, fw["ffn2_w1"], fw["ffn2_w3"], fw["ffn2_w2"])

        with contextlib.ExitStack() as es:
            gfc = sbt(es, "gfc", [128, KT], F32)
            P.dma("sp", gfc[:], gfin[:], writes=["gfinc"])
            identf2 = sbt(es, "identf2", [128, 128], F32)
            P.dma("sp", identf2[:], cin["ident"][:], writes=["identf2"])
            uF = sbt(es, "uF", [128, KT, 512], F32)
            hch = sbt(es, "hchF", [128, KT, 512], F32)
            sq = sbt(es, "sqF", [128, KT, 512], BF16)
            rstd = sbt(es, "rstdF", [128, 512], F32)
            ost = [sbt(es, "ost%d" % i, [128, D], F32) for i in range(2)]
            oi = 0
            for c in range(T // 512):
                norm_mod((hch, sq, rstd), 0, c * 512, 512, uF, "uF", a_ap=gfc)
                for q in range(4):
                    ob = ost[oi % 2]; ok = ("ost", oi % 2); oi += 1
                    for g4 in range(4):
                        pb, pk = bank()
                        for r in range(4):
                            kt = g4 * 4 + r
                            P.op("pe", lambda e, pb=pb, r=r, kt=kt, q=q: e.transpose(out=pb[:, r * 128:(r + 1) * 128], in_=uF[:, kt, q * 128:(q + 1) * 128], identity=identf2[:]),
                                 reads=["uF", "identf2"], writes=[pk], inc=(r == 3))
                        if g4 % 2:
                            P.op("act", lambda e, pb=pb, ob=ob, g4=g4: e.copy(out=ob[:, g4 * 512:(g4 + 1) * 512], in_=pb[:]), reads=[pk], writes=[ok + (g4,)])
                        else:
                            P.op("dve", lambda e, pb=pb, ob=ob, g4=g4: e.tensor_copy(out=ob[:, g4 * 512:(g4 + 1) * 512], in_=pb[:]), reads=[pk], writes=[ok + (g4,)])
                    r0 = c * 512 + q * 128
                    P.dma("sp", out[r0:r0 + 128, :], ob[:], reads=[ok + (g,) for g in range(4)])
        if dbg:
            P.barrier()
            P.dma("sp", taps["yT"][:], yT[:], reads=[("yT", n) for n in range(NT)])
            P.dma("sp", taps["hT"][:], hT[:], reads=[("hT", k, n) for k in range(KT) for n in range(NT)])
        P.emit()
    return nc


def col(v):
    v = np.asarray(v, dtype=np.float32)
    return np.ascontiguousarray(v.reshape(-1, 128).T)


def prep_shared(inp, L, T):
    f32 = np.float32
    m = {}
    for k in ("w_ada", "b_ada", "ffn1_w1", "ffn1_w3", "ffn1_w2", "ffn2_w1", "ffn2_w3", "ffn2_w2", "w_in", "w_out"):
        m[k] = np.ascontiguousarray(np.asarray(inp[k], dtype=f32)[:L])
    m["gcols"] = np.stack([np.stack([col(inp[k][l]) for k in ("g_ffn1", "g_mix", "g_ffn2")]) for l in range(L)])
    m["gfin"] = col(inp["g_final"])
    lam_re = np.asarray(inp["s5_lam_re"], f32)[:L]
    lam_im = np.asarray(inp["s5_lam_im"], f32)[:L]
    ldt = np.asarray(inp["s5_log_dt"], f32)[:L]
    def stl(a):
        return a.reshape(L, 2, 16, 128).transpose(0, 1, 3, 2)
    ldtb = np.broadcast_to(ldt[..., None], (L, 2, 32, 64))
    m["s5p"] = np.ascontiguousarray(np.stack([stl(lam_re), stl(lam_im), stl(ldtb)], axis=3)).astype(f32)
    b_re = np.asarray(inp["s5_b_re"], f32)[:L]; b_im = np.asarray(inp["s5_b_im"], f32)[:L]
    c_re = np.asarray(inp["s5_c_re"], f32)[:L]; c_im = np.asarray(inp["s5_c_im"], f32)[:L]
    Bm = np.zeros((L, 2, 2, 16, 128, 128), f32)
    Cm = np.zeros((L, 2, 2, 16, 128, 128), f32)
    for g in range(32):
        st = g // 2; so = (g % 2) * 64
        co = (g % 8) * 16
        for ri, (bb, cc) in enumerate(((b_re, c_re), (b_im, c_im))):
            Bm[:, :, ri, st, co:co + 16, so:so + 64] = bb[:, :, g].transpose(0, 1, 3, 2)
            Cm[:, :, ri, st, so:so + 64, co:co + 16] = cc[:, :, g].transpose(0, 1, 3, 2)
    m["s5B"] = Bm; m["s5C"] = Cm
    m["s5d"] = np.stack([col(inp["s5_d"][l]) for l in range(L)])
    m["wglu"] = np.ascontiguousarray(np.asarray(inp["s5_w_glu"], f32)[:L])
    m["bglu"] = np.stack([col(inp["s5_b_glu"][l]) for l in range(L)])
    wg = np.asarray(inp["gla_w_gate"], f32)[:L]
    wgb = np.zeros((L, 32, 512), f32)
    wgb[:, 0:16, 0:256] = wg[:, 0]
    wgb[:, 16:32, 256:512] = wg[:, 1]
    m["wgate"] = wgb
    m["bgate"] = np.ascontiguousarray(np.asarray(inp["gla_b_gate"], f32)[:L].reshape(L, 1, 512))
    for k, v in host_consts(T).items():
        m["c_" + k] = v
    return m


def prep_core(inp, shared, b):
    m = dict(shared)
    m["x"] = np.ascontiguousarray(np.asarray(inp["x"][b], dtype=np.float32))
    m["cT"] = col(inp["c"][b])
    return m


SEQ = 4096
DEPTH = 4
N_CORES = 8
_NC_CACHE = {}


def kernel(**inputs):
    inp = {k: np.asarray(v) for k, v in inputs.items()}
    nb = inp["x"].shape[0]
    shared = prep_shared(inp, DEPTH, SEQ)
    hot = [0, 2, 4, 6][:nb]
    real = {c: prep_core(inp, shared, i) for i, c in enumerate(hot)}
    zmap = None
    in_maps = []
    for c in range(N_CORES):
        if c in real:
            in_maps.append(real[c])
        else:
            if zmap is None:
                zmap = {k: np.zeros_like(v) for k, v in real[hot[0]].items()}
            in_maps.append(zmap)
    if "nc" not in _NC_CACHE:
        _NC_CACHE["nc"] = build(SEQ, DEPTH)
    res = run_bass_kernel_spmd(_NC_CACHE["nc"], in_maps, core_ids=list(range(N_CORES)))
    out = np.stack([np.asarray(res.results[c]["out"]) for c in hot], axis=0)
    return out.astype(np.float32)
```

```python
import contextlib, math, os
import numpy as np
import concourse.bass as bass
import concourse.mybir as mybir
from concourse.bass_utils import run_bass_kernel_spmd


ENGS = ("pe", "act", "dve", "pool", "sp")
N_DMA_SEMS = 6


class Prog:
    def __init__(self, nc):
        self.nc = nc
        self.ops = {e: [] for e in ENGS}
        self.cnt = {e: 0 for e in ENGS}
        self.pending = {e: False for e in ENGS}
        self.seen = {e: {} for e in ENGS}
        self.state = {}
        self.dma_cnt = {}
        self.dma_rr = {e: 0 for e in ENGS}
        self.n_ops = 0

    def _deps(self, eng, reads, writes):
        need = {}
        def add(tok, raw):
            if tok is None:
                return
            sk, v = tok
            if sk == eng and not raw:
                return
            if sk == eng and eng == "pe":
                return
            if need.get(sk, 0) < v:
                need[sk] = v
        for k in reads:
            st = self.state.get(k)
            if st:
                add(st[0], True)
                if isinstance(k, tuple) and k[0] in ("pb", "pt"):
                    for sk, v in st[1].items():
                        add((sk, v), False)
        for k in writes:
            st = self.state.get(k)
            if st:
                add(st[0], False)
                for sk, v in st[1].items():
                    add((sk, v), False)
        out = []
        for sk, v in need.items():
            if self.seen[eng].get(sk, 0) < v:
                self.seen[eng][sk] = v
                out.append((sk, v))
        return out

    def _commit(self, tok, reads, writes):
        for k in reads:
            st = self.state.setdefault(k, [None, {}])
            sk, v = tok
            if st[1].get(sk, 0) < v:
                st[1][sk] = v
        for k in writes:
            self.state[k] = [tok, {}]

    def op(self, eng, fn, reads=(), writes=(), inc=True):
        waits = self._deps(eng, reads, writes)
        if inc:
            self.cnt[eng] += 1
            tok = (eng, self.cnt[eng])
            self.pending[eng] = False
        else:
            tok = (eng, self.cnt[eng] + 1)
            self.pending[eng] = True
        self.ops[eng].append((waits, fn, eng if inc else None))
        self._commit(tok, reads, writes)
        self.n_ops += 1
        return tok

    def dma(self, q, out, in_, reads=(), writes=(), **kw):
        waits = self._deps(q, reads, writes)
        i = self.dma_rr[q]
        self.dma_rr[q] = (i + 1) % N_DMA_SEMS
        sk = ("d", q, i)
        self.dma_cnt[sk] = self.dma_cnt.get(sk, 0) + 16
        tok = (sk, self.dma_cnt[sk])
        self.ops[q].append((waits, lambda e, o=out, n=in_, kw=kw: e.dma_start(out=o, in_=n, **kw), sk))
        self._commit(tok, reads, writes)
        self.n_ops += 1
        return tok

    def barrier(self):
        assert not any(self.pending.values()), self.pending
        allt = [(sk, v) for sk, v in list(self.cnt.items()) + list(self.dma_cnt.items()) if v > 0]
        for e in ENGS:
            waits = []
            for sk, v in allt:
                if sk == e:
                    continue
                if self.seen[e].get(sk, 0) < v:
                    self.seen[e][sk] = v
                    waits.append((sk, v))
            if waits:
                self.ops[e].append((waits, None, None))
        self.state = {}

    def emit(self, final_waits_eng="sp"):
        nc = self.nc
        assert not any(self.pending.values()), self.pending
        with contextlib.ExitStack() as es:
            sems = {}
            for e in ENGS:
                sems[e] = es.enter_context(nc.semaphore("s_" + e))
            for q in ("sp", "act", "pool"):
                for i in range(N_DMA_SEMS):
                    sems[("d", q, i)] = es.enter_context(nc.semaphore("d_%s_%d" % (q, i)))
            block = es.enter_context(nc.Block())
            fin = []
            for sk, v in list(self.cnt.items()) + list(self.dma_cnt.items()):
                if v > 0 and sk != final_waits_eng:
                    fin.append((sk, v))

            def run(eng_name):
                def body(e):
                    for waits, fn, inc in self.ops[eng_name]:
                        for sk, v in waits:
                            e.wait_ge(sems[sk], v)
                        if fn is None:
                            continue
                        ins = fn(e)
                        if inc is not None:
                            ins.then_inc(sems[inc], 16 if isinstance(inc, tuple) else 1)
                    if eng_name == final_waits_eng:
                        for sk, v in fin:
                            e.wait_ge(sems[sk], v)
                return body

            block.tensor(run("pe"))
            block.scalar(run("act"))
            block.vector(run("dve"))
            block.gpsimd(run("pool"))
            block.sync(run("sp"))


F32 = mybir.dt.float32
BF16 = mybir.dt.bfloat16
AF = mybir.ActivationFunctionType
ALU = mybir.AluOpType
TWO_PI = 2.0 * math.pi


def s5_phase(B, l):
    P = B.P; T = B.T; sbt = B.sbt; U = B.U; bank = B.bank
    suT = B.suT; yT = B.yT
    NC5 = T // 512
    NLEV = int(round(math.log2(T)))
    assert 2 ** NLEV == T

    def tt_op(eng, out, in0, in1, op, reads, writes):
        P.op(eng, lambda e: e.tensor_tensor(out=out, in0=in0, in1=in1, op=op), reads=reads, writes=writes)

    def chunk(c, rev):
        if not rev:
            return slice(c * 512, (c + 1) * 512)
        hi = T - 1 - c * 512
        lo = T - 1 - (c + 1) * 512
        return slice(hi, None if lo < 0 else lo, -1)

    with contextlib.ExitStack() as es:
        par = sbt(es, U("s5par"), [128, 2, 3, 16], F32)
        P.dma("sp", par[:], B.s5p[l].rearrange("d p v s -> p d v s"), writes=["par"])
        sc = sbt(es, U("s5sc"), [128, 20, 2, 16], F32)
        SL = {nm: i for i, nm in enumerate(["dt", "redt", "th", "rho", "k1", "k2", "r", "sin", "cos", "lr", "li", "nr", "den", "t1", "t2", "cre", "cim", "ncim", "th2", "k3"])}
        S = lambda nm: sc[:, SL[nm]]
        K = lambda nm: ("s5sc", nm)

        def ts(out, in0, s1, op0, reads, writes, s2=None, op1=None):
            if op1 is None:
                P.op("dve", lambda e: e.tensor_scalar(out=out, in0=in0, scalar1=s1, scalar2=None, op0=op0), reads=reads, writes=writes)
            else:
                P.op("dve", lambda e: e.tensor_scalar(out=out, in0=in0, scalar1=s1, scalar2=s2, op0=op0, op1=op1), reads=reads, writes=writes)

        P.op("act", lambda e: e.activation(out=S("dt"), in_=par[:, :, 2, :], func=AF.Exp), reads=["par"], writes=[K("dt")])
        tt_op("dve", S("redt"), par[:, :, 0, :], S("dt"), ALU.mult, ["par", K("dt")], [K("redt")])
        tt_op("dve", S("th"), par[:, :, 1, :], S("dt"), ALU.mult, ["par", K("dt")], [K("th")])
        P.op("act", lambda e: e.activation(out=S("rho"), in_=S("redt"), func=AF.Exp), reads=[K("redt")], writes=[K("rho")])

        def sin_of(src_nm, dst_nm):
            ts(S("k1"), S(src_nm), TWO_PI, ALU.is_ge, [K(src_nm)], [K("k1")])
            ts(S("k2"), S(src_nm), 2 * TWO_PI, ALU.is_ge, [K(src_nm)], [K("k2")])
            ts(S("k3"), S(src_nm), 3 * TWO_PI, ALU.is_ge, [K(src_nm)], [K("k3")])
            tt_op("dve", S("k1"), S("k1"), S("k2"), ALU.add, [K("k1"), K("k2")], [K("k1")])
            tt_op("dve", S("k1"), S("k1"), S("k3"), ALU.add, [K("k1"), K("k3")], [K("k1")])
            P.op("dve", lambda e: e.scalar_tensor_tensor(out=S("r"), in0=S("k1"), scalar=-TWO_PI, in1=S(src_nm), op0=ALU.mult, op1=ALU.add), reads=[K("k1"), K(src_nm)], writes=[K("r")])
            ts(S("r"), S("r"), 0.0, ALU.max, [K("r")], [K("r")], s2=TWO_PI, op1=ALU.min)
            P.op("act", lambda e: e.activation(out=S(dst_nm), in_=S("r"), func=AF.Sin, scale=-1.0, bias=B.pic[:]), reads=[K("r"), "pic"], writes=[K(dst_nm)])

        sin_of("th", "sin")
        ts(S("th2"), S("th"), math.pi / 2, ALU.add, [K("th")], [K("th2")])
        sin_of("th2", "cos")
        tt_op("dve", S("lr"), S("rho"), S("cos"), ALU.mult, [K("rho"), K("cos")], [K("lr")])
        tt_op("dve", S("li"), S("rho"), S("sin"), ALU.mult, [K("rho"), K("sin")], [K("li")])
        ts(S("nr"), S("lr"), -1.0, ALU.add, [K("lr")], [K("nr")])
        tt_op("dve", S("den"), par[:, :, 0, :], par[:, :, 0, :], ALU.mult, ["par"], [K("den")])
        tt_op("dve", S("t1"), par[:, :, 1, :], par[:, :, 1, :], ALU.mult, ["par"], [K("t1")])
        tt_op("dve", S("den"), S("den"), S("t1"), ALU.add, [K("den"), K("t1")], [K("den")])
        P.op("dve", lambda e: e.reciprocal(out=S("den"), in_=S("den")), reads=[K("den")], writes=[K("den")])
        tt_op("dve", S("t1"), S("nr"), par[:, :, 0, :], ALU.mult, [K("nr"), "par"], [K("t1")])
        tt_op("dve", S("t2"), S("li"), par[:, :, 1, :], ALU.mult, [K("li"), "par"], [K("t2")])
        tt_op("dve", S("t1"), S("t1"), S("t2"), ALU.add, [K("t1"), K("t2")], [K("t1")])
        tt_op("dve", S("cre"), S("t1"), S("den"), ALU.mult, [K("t1"), K("den")], [K("cre")])
        tt_op("dve", S("t1"), S("li"), par[:, :, 0, :], ALU.mult, [K("li"), "par"], [K("t1")])
        tt_op("dve", S("t2"), S("nr"), par[:, :, 1, :], ALU.mult, [K("nr"), "par"], [K("t2")])
        tt_op("dve", S("t1"), S("t1"), S("t2"), ALU.subtract, [K("t1"), K("t2")], [K("t1")])
        tt_op("dve", S("cim"), S("t1"), S("den"), ALU.mult, [K("t1"), K("den")], [K("cim")])
        ts(S("ncim"), S("cim"), -1.0, ALU.mult, [K("cim")], [K("ncim")])

        su_ct = sbt(es, U("su_ct"), [128, T], BF16)
        dcol = sbt(es, U("s5dcol"), [128, 4], F32)
        P.dma("sp", dcol[:], B.s5d[l], writes=["s5dcol"])
        bgl = sbt(es, U("bgl"), [128, 4], F32)
        P.dma("sp", bgl[:], B.bglu[l], writes=["bgl"])
        wgl = sbt(es, U("wgl"), [128, 4, 512], BF16)
        P.dma("pool", wgl[:], B.wglu[l].rearrange("(k p) n -> p k n", p=128), writes=["wgl"])
        TC = min(T, 1024)
        sA = sbt(es, U("sA"), [128, TC], F32)
        sI = sbt(es, U("sI"), [128, TC], mybir.dt.int32)
        hB = sbt(es, U("hB"), [128, TC], F32)
        U1 = sbt(es, U("U1"), [128, 32, T // 64], F32)
        U2 = sbt(es, U("U2"), [128, 32, 64], F32)
        U1i = sbt(es, U("U1i"), [128, 32, 64], mybir.dt.int32)
        io64 = sbt(es, U("io64"), [128, 64], F32)
        P.dma("sp", io64[:], B.cin["iota64"][:], writes=["io64"])
        fq = sc[:, SL["k2"]].rearrange("p d s -> p (d s)"); gq = sc[:, SL["k3"]].rearrange("p d s -> p (d s)")
        ts(S("k2"), S("th"), 1.0 / TWO_PI, ALU.mult, [K("th")], [K("k2")])
        ts(S("k3"), S("k2"), 64.0, ALU.mult, [K("k2")], [K("k3")])
        P.op("dve", lambda e: e.tensor_copy(out=U1i[:, :, 0], in_=gq), reads=[K("k3")], writes=["U1i"])
        tt_op("dve", gq, gq, U1i[:, :, 0], ALU.subtract, [K("k3"), "U1i"], [K("k3")])
        NA = T // 64
        tt_op("dve", U1[:], gq.unsqueeze(2).broadcast_to([128, 32, NA]), io64[:, 0:NA].unsqueeze(1).broadcast_to([128, 32, NA]), ALU.mult, [K("k3"), "io64"], ["U1"])
        P.op("dve", lambda e: e.tensor_copy(out=U1i[:, :, 0:NA], in_=U1[:]), reads=["U1"], writes=["U1i"])
        tt_op("dve", U1[:], U1[:], U1i[:, :, 0:NA], ALU.subtract, ["U1", "U1i"], ["U1"])
        tt_op("dve", U2[:], fq.unsqueeze(2).broadcast_to([128, 32, 64]), io64[:].unsqueeze(1).broadcast_to([128, 32, 64]), ALU.mult, [K("k2"), "io64"], ["U2"])
        P.op("dve", lambda e: e.tensor_copy(out=U1i[:], in_=U2[:]), reads=["U2"], writes=["U1i"])
        tt_op("dve", U2[:], U2[:], U1i[:], ALU.subtract, ["U2", "U1i"], ["U2"])
        zz = sbt(es, U("zz"), [128, 2, T], BF16)
        csbs = [sbt(es, U("csb"), [128, 2, T], BF16) for _ in range(2)]
        POOL_LAST = os.environ.get("S5_POOL_LAST", "0") == "1"
        RHO_BCAST = os.environ.get("S5_RHO_BCAST", "1") == "1"
        if not RHO_BCAST:
            rho_t = sbt(es, U("rho_t"), [128, T], F32)
        xx = sbt(es, U("xx"), [128, 2, T], BF16)
        y_ct = sbt(es, U("y_ct"), [128, T], F32)
        zT = sbt(es, U("zT"), [128, 4, T], BF16)
        Bb = [sbt(es, U("Bb"), [128, 2, 128], BF16) for _ in range(2)]
        Cf = [sbt(es, U("Cf"), [128, 2, 128], F32) for _ in range(2)]
        Cp = [sbt(es, U("Cp"), [128, 2, 128], BF16) for _ in range(2)]
        ctmp = sbt(es, U("ctmp"), [128, 128], F32)
        tmp = [sbt(es, U("s5tmp"), [128, 512], F32) for _ in range(4)]
        tmpb = [sbt(es, U("s5tmpb"), [128, 512], BF16) for _ in range(12)]
        tbi = [0]

        def TB_():
            i = tbi[0] % 12; tbi[0] += 1
            return tmpb[i], ("s5tmpb", i)

        ti = [0]

        def T_():
            i = ti[0] % 4; ti[0] += 1
            return tmp[i], ("s5tmp", i)

        def gen_tables(d, st, csb, ckey):
            dsi = d * 16 + st
            for h2 in range(T // TC):
                a0 = h2 * (TC // 64); a1 = (h2 + 1) * (TC // 64)
                na = a1 - a0
                sA3 = sA[:].rearrange("p (a b) -> p a b", b=64)
                tt_op("dve", sA3, U1[:, dsi, a0:a1].unsqueeze(2).broadcast_to([128, na, 64]), U2[:, dsi, :].unsqueeze(1).broadcast_to([128, na, 64]), ALU.add, ["U1", "U2"], ["sA"])
                P.op("dve", lambda e: e.tensor_copy(out=sI[:], in_=sA[:]), reads=["sA"], writes=["sI"])
                tt_op("dve", sA[:], sA[:], sI[:], ALU.subtract, ["sA", "sI"], ["sA"])
                tsl = slice(h2 * TC, (h2 + 1) * TC)
                P.op("act", lambda e, tsl=tsl, csb=csb: e.activation(out=csb[:, 1, tsl], in_=sA[:], func=AF.Sin, scale=TWO_PI), reads=["sA"], writes=[ckey])
                P.op("act", lambda e: e.activation(out=hB[:], in_=sA[:], func=AF.Sin, scale=math.pi), reads=["sA"], writes=["hB"])
                P.op("act", lambda e: e.activation(out=hB[:], in_=hB[:], func=AF.Square), reads=["hB"], writes=["hB"])
                P.op("act", lambda e, tsl=tsl, csb=csb: e.activation(out=csb[:, 0, tsl], in_=hB[:], func=AF.Identity, scale=-2.0, bias=1.0), reads=["hB"], writes=[ckey])
            pass

        iters = [(ct_, d_, st_) for ct_ in range(4) for d_ in range(2) for st_ in range(ct_ * 4, ct_ * 4 + 4)]
        gen_tables(iters[0][1], iters[0][2], csbs[0], ("csb", 0))
        it = 0
        for ct in range(4):
            P.dma("sp", su_ct[:], suT[ct], reads=[("suT", ct, c) for c in range(NC5)], writes=["su_all"])
            ts(y_ct[:], su_ct[:], dcol[:, ct:ct + 1], ALU.mult, ["su_all", "s5dcol"], ["y_ct"])
            for d in range(2):
                rev = (d == 1)
                for st in range(ct * 4, ct * 4 + 4):
                    b = it % 2; it += 1
                    P.dma("pool", Bb[b][:], B.s5B[l, d, :, st].rearrange("r k m -> k r m"), writes=[("Bb", b)])
                    P.dma("sp", Cf[b][:], B.s5C[l, d, :, st].rearrange("r k m -> k r m"), writes=[("Cf", b)])
                    cre = sc[:, SL["cre"], d, st:st + 1]; cim = sc[:, SL["cim"], d, st:st + 1]; ncim = sc[:, SL["ncim"], d, st:st + 1]
                    ts(ctmp[:], Cf[b][:, 1, :], cim, ALU.mult, [("Cf", b), K("cim")], ["ctmp"])
                    P.op("dve", lambda e, b=b, cre=cre: e.scalar_tensor_tensor(out=Cp[b][:, 0, :], in0=Cf[b][:, 0, :], scalar=cre, in1=ctmp[:], op0=ALU.mult, op1=ALU.subtract), reads=[("Cf", b), K("cre"), "ctmp"], writes=[("Cp", b, 0)])
                    ts(ctmp[:], Cf[b][:, 1, :], cre, ALU.mult, [("Cf", b), K("cre")], ["ctmp"])
                    P.op("dve", lambda e, b=b, ncim=ncim: e.scalar_tensor_tensor(out=Cp[b][:, 1, :], in0=Cf[b][:, 0, :], scalar=ncim, in1=ctmp[:], op0=ALU.mult, op1=ALU.subtract), reads=[("Cf", b), K("ncim"), "ctmp"], writes=[("Cp", b, 1)])
                    csb = csbs[(it - 1) % 2]; ckey = ("csb", (it - 1) % 2)
                    csk = []
                    rho_col = sc[:, SL["rho"], d, st:st + 1]
                    if RHO_BCAST:
                        rho_ap = rho_col.broadcast_to([128, T]); rho_rd = [K("rho")]
                    else:
                        P.op("act", lambda e, rho_col=rho_col: e.activation(out=rho_t[:], in_=zz[:, 0, :], func=AF.Identity, scale=0.0, bias=rho_col), reads=[K("rho")], writes=["rho_t"])
                        rho_ap = rho_t[:]; rho_rd = ["rho_t"]
                    for c in range(NC5):
                        sl = slice(c * 512, (c + 1) * 512)
                        pA, kAp = bank(); pB, kBp = bank()
                        rhs = su_ct[:, chunk(c, rev)]
                        P.op("pe", lambda e, pA=pA, b=b, rhs=rhs: e.matmul(pA[:], lhsT=Bb[b][:, 0, :], rhs=rhs, start=True, stop=True), reads=[("Bb", b), "su_all"], writes=[kAp])
                        P.op("pe", lambda e, pB=pB, b=b, rhs=rhs: e.matmul(pB[:], lhsT=Bb[b][:, 1, :], rhs=rhs, start=True, stop=True), reads=[("Bb", b), "su_all"], writes=[kBp])
                        ab, kab = TB_(); bb_, kbb = TB_()
                        P.op("act", lambda e, pA=pA, ab=ab: e.copy(out=ab[:], in_=pA[:]), reads=[kAp], writes=[kab])
                        P.op("act", lambda e, pB=pB, bb_=bb_: e.copy(out=bb_[:], in_=pB[:]), reads=[kBp], writes=[kbb])
                        t1, k1 = TB_(); t2, k2 = TB_(); t3, k3 = TB_(); t4, k4 = TB_()
                        tt_op("dve", t1[:], ab[:], csb[:, 0, sl], ALU.mult, [kab, ckey], [k1])
                        tt_op("pool", t2[:], bb_[:], csb[:, 1, sl], ALU.mult, [kbb, ckey], [k2])
                        tt_op("dve", zz[:, 0, sl], t1[:], t2[:], ALU.add, [k1, k2], [("zz", 0, c)])
                        tt_op("pool", t3[:], bb_[:], csb[:, 0, sl], ALU.mult, [kbb, ckey], [k3])
                        tt_op("dve", t4[:], ab[:], csb[:, 1, sl], ALU.mult, [kab, ckey], [k4])
                        tt_op("dve", zz[:, 1, sl], t3[:], t4[:], ALU.subtract, [k3, k4], [("zz", 1, c)])
                    if it < len(iters):
                        gen_tables(iters[it][1], iters[it][2], csbs[it % 2], ("csb", it % 2))
                    for ri in range(2):
                        zk = [("zz", ri, c) for c in range(NC5)]
                        P.op("dve", lambda e, ri=ri, rho_ap=rho_ap: e.tensor_tensor_scan(out=zz[:, ri, :], data0=rho_ap, data1=zz[:, ri, :], initial=0.0, op0=ALU.mult, op1=ALU.add), reads=zk + rho_rd, writes=zk)
                    for c in range(NC5):
                        sl = slice(c * 512, (c + 1) * 512)
                        t1, k1 = TB_(); t2, k2 = TB_(); t3, k3 = TB_(); t4, k4 = TB_()
                        tt_op("dve", t1[:], zz[:, 0, sl], csb[:, 0, sl], ALU.mult, [("zz", 0, c), ckey], [k1])
                        tt_op("pool", t2[:], zz[:, 1, sl], csb[:, 1, sl], ALU.mult, [("zz", 1, c), ckey], [k2])
                        tt_op("dve", xx[:, 0, sl], t1[:], t2[:], ALU.subtract, [k1, k2], [("xx", c)])
                        tt_op("pool", t3[:], zz[:, 1, sl], csb[:, 0, sl], ALU.mult, [("zz", 1, c), ckey], [k3])
                        tt_op("dve", t4[:], zz[:, 0, sl], csb[:, 1, sl], ALU.mult, [("zz", 0, c), ckey], [k4])
                        tt_op("pool", xx[:, 1, sl], t3[:], t4[:], ALU.add, [k3, k4], [("xx", c)])
                    for c in range(NC5):
                        cc = c if not rev else NC5 - 1 - c
                        pY, kY = bank()
                        sl = slice(c * 512, (c + 1) * 512)
                        if not rev:
                            r_re = xx[:, 0, sl]; r_im = xx[:, 1, sl]
                        else:
                            r_re = xx[:, 0, chunk(c, True)]; r_im = xx[:, 1, chunk(c, True)]
                        P.op("pe", lambda e, pY=pY, b=b, r_re=r_re: e.matmul(pY[:], lhsT=Cp[b][:, 0, :], rhs=r_re, start=True, stop=False), reads=[("Cp", b, 0), ("xx", cc)], writes=[kY], inc=False)
                        P.op("pe", lambda e, pY=pY, b=b, r_im=r_im: e.matmul(pY[:], lhsT=Cp[b][:, 1, :], rhs=r_im, start=False, stop=True), reads=[("Cp", b, 1), ("xx", cc)], writes=[kY])
                        tt_op("dve", y_ct[:, sl], pY[:], y_ct[:, sl], ALU.add, [kY, "y_ct"], ["y_ct"])
            for c in range(NC5):
                sl = slice(c * 512, (c + 1) * 512)
                t1, k1 = T_(); t2, k2 = T_()
                tt_op("dve", t1[:], y_ct[:, sl], y_ct[:, sl], ALU.mult, ["y_ct"], [k1])
                ts(t1[:], t1[:], 0.044715, ALU.mult, [k1], [k1], s2=1.0, op1=ALU.add)
                tt_op("dve", t2[:], t1[:], y_ct[:, sl], ALU.mult, [k1, "y_ct"], [k2])
                P.op("act", lambda e, t2=t2: e.activation(out=t2[:], in_=t2[:], func=AF.Sigmoid, scale=2.0 * math.sqrt(2.0 / math.pi)), reads=[k2], writes=[k2])
                tt_op("dve", zT[:, ct, sl], t2[:], y_ct[:, sl], ALU.mult, [k2, "y_ct"], [("zT", ct, c)])
        ystg = [sbt(es, U("s5ystg"), [128, 512], BF16) for _ in range(2)]
        yi = 0
        for m in range(4):
            for c in range(NC5):
                sl = slice(c * 512, (c + 1) * 512)
                pb, pk = bank()
                for kt in range(4):
                    P.op("pe", lambda e, pb=pb, kt=kt, m=m, sl=sl: e.matmul(pb[:], lhsT=wgl[:, kt, m * 128:(m + 1) * 128], rhs=zT[:, kt, sl], start=(kt == 0), stop=(kt == 3)),
                         reads=["wgl"] + [("zT", kt, c)], writes=[pk], inc=(kt == 3))
                t1, k1 = T_()
                P.op("act", lambda e, pb=pb, t1=t1, m=m: e.activation(out=t1[:], in_=pb[:], func=AF.Sigmoid, bias=bgl[:, m:m + 1], scale=1.0), reads=[pk, "bgl"], writes=[k1])
                ys = ystg[yi % 2]; yk = ("s5ystg", yi % 2); yi += 1
                tt_op("dve", ys[:], t1[:], zT[:, m, sl], ALU.mult, [k1, ("zT", m, c)], [yk])
                P.dma("sp", yT[8 + m, :, sl], ys[:], reads=[yk], writes=[("yT", 1, c * 4 + j) for j in range(4)])


F32 = mybir.dt.float32
BF16 = mybir.dt.bfloat16
AF = mybir.ActivationFunctionType
ALU = mybir.AluOpType
AX = mybir.AxisListType
KT = 16
D = 2048
EPS = 1e-6
LOG_GAMMA = [math.log1p(-2.0 ** (-5.0 - h)) for h in range(4)]
GAMMA_C = [math.exp(128.0 * g) for g in LOG_GAMMA]
TWO_PI = 2.0 * math.pi


def mixer(B, l):
    P = B.P; nc = B.nc; T = B.T; NT = B.NT; TT = B.TT; NTT = B.NTT; NCH = B.NCH
    sbt = B.sbt; U = B.U; bank = B.bank; bank2 = B.bank2; bankb = B.bankb
    hT = B.hT; proj = B.proj; suT = B.suT; glrT = B.glrT; yT = B.yT
    ident = B.ident; onesf = B.onesf; epsc = B.epsc; g05 = B.g05
    cin = B.cin
    NC5 = T // 512
    alt = [0]

    def evac(out_ap, in_ap, reads, writes):
        alt[0] ^= 1
        if alt[0]:
            P.op("act", lambda e: e.copy(out=out_ap, in_=in_ap), reads=reads, writes=writes)
        else:
            P.op("dve", lambda e: e.tensor_copy(out=out_ap, in_=in_ap), reads=reads, writes=writes)

    def tt_op(eng, out, in0, in1, op, reads, writes):
        P.op(eng, lambda e: e.tensor_tensor(out=out, in0=in0, in1=in1, op=op), reads=reads, writes=writes)

    def transposes(src_fn, nblk, dst, dkey, reads):
        pt, pk = bankb()
        for i in range(nblk):
            P.op("pe", lambda e, i=i: e.transpose(out=pt[:, i * 128:(i + 1) * 128], in_=src_fn(i), identity=ident[:]), reads=list(reads) + ["ident"], writes=[pk], inc=(i == nblk - 1))
        return pt, pk

    with contextlib.ExitStack() as es:
        uT = sbt(es, U("muT"), [128, KT, TT], BF16)
        hch = sbt(es, U("mhch"), [128, KT, 512], F32)
        sq = sbt(es, U("msq"), [128, KT, 512], BF16)
        rstd = sbt(es, U("mrstd"), [128, 512], F32)
        wch = [sbt(es, U("wch"), [128, KT, 512], BF16) for _ in range(2)]
        wg32 = sbt(es, U("wg32"), [128, KT, 32], BF16)
        stg = [sbt(es, U("stg"), [128, 512], BF16) for _ in range(4)]
        stgf = [sbt(es, U("stgf"), [32, 512], F32) for _ in range(2)]
        P.dma("pool", wg32[:], B.w_in[l, :, 5120:5152].rearrange("(k p) n -> p k n", p=128), writes=["wg32"])
        wi = 0; si = 0; sfi = 0
        for tt in range(NTT):
            t0 = tt * TT
            B.norm_mod((hch, sq, rstd), 1, t0, TT, uT, "uT")
            for cc in range(10):
                wb = wch[wi % 2]; wk = ("wch", wi % 2); wi += 1
                P.dma("pool", wb[:], B.w_in[l, :, cc * 512:(cc + 1) * 512].rearrange("(k p) n -> p k n", p=128), writes=[wk])
                for n in range(TT // 128):
                    pb, pk = bank()
                    for kt in range(KT):
                        P.op("pe", lambda e, pb=pb, kt=kt, n=n, wb=wb: e.matmul(pb[:], lhsT=uT[:, kt, n * 128:(n + 1) * 128], rhs=wb[:, kt, :], start=(kt == 0), stop=(kt == KT - 1)),
                             reads=["uT", wk], writes=[pk], inc=(kt == KT - 1))
                    sb_ = stg[si % 4]; sk = ("stg", si % 4); si += 1
                    evac(sb_[:], pb[:], [pk], [sk])
                    r0 = t0 + n * 128
                    P.dma("sp", proj[r0:r0 + 128, cc * 512:(cc + 1) * 512], sb_[:], reads=[sk], writes=[("proj", r0 // 128, cc)])
                if cc == 6:
                    for m in range(4):
                        for c in range(NCH):
                            pb, pk = bank()
                            for kt in range(KT):
                                P.op("pe", lambda e, pb=pb, kt=kt, m=m, c=c, wb=wb: e.matmul(pb[:], lhsT=wb[:, kt, m * 128:(m + 1) * 128], rhs=uT[:, kt, c * 512:(c + 1) * 512], start=(kt == 0), stop=(kt == KT - 1)),
                                     reads=["uT", wk], writes=[pk], inc=(kt == KT - 1))
                            sb_ = stg[si % 4]; sk = ("stg", si % 4); si += 1
                            evac(sb_[:], pb[:], [pk], [sk])
                            c0 = t0 + c * 512
                            P.dma("sp", suT[m, :, c0:c0 + 512], sb_[:], reads=[sk], writes=[("suT", m, c0 // 512)])
            for c in range(NCH):
                pb, pk = bank()
                for kt in range(KT):
                    P.op("pe", lambda e, pb=pb, kt=kt, c=c: e.matmul(pb[0:32, :], lhsT=wg32[:, kt, :], rhs=uT[:, kt, c * 512:(c + 1) * 512], start=(kt == 0), stop=(kt == KT - 1)),
                         reads=["uT", "wg32"], writes=[pk], inc=(kt == KT - 1))
                sf = stgf[sfi % 2]; sfk = ("stgf", sfi % 2); sfi += 1
                evac(sf[:], pb[0:32, :], [pk], [sfk])
                c0 = t0 + c * 512
                P.dma("sp", glrT[:, c0:c0 + 512], sf[:], reads=[sfk], writes=[("glrT", c0 // 512)])
    P.barrier()

    SKIP = os.environ.get("MIXSKIP", "").split(",")
    def r_gen(es):
     if True:
      if "R" not in SKIP:
            ropec = sbt(es, U("ropec"), [128, NT, 64], F32)
            ropes = sbt(es, U("ropes"), [128, NT, 64], F32)
            rmask = sbt(es, U("rmask"), [128, 4, 128], F32)
            ftab = sbt(es, U("ftab"), [128, 4, 128], F32)
            btab = sbt(es, U("btab"), [128, 4, 128], F32)
            kfb = sbt(es, U("kfb"), [128, 2, 4], F32)
            for nm, tl in (("ropec", ropec), ("ropes", ropes), ("rmask", rmask), ("ftab", ftab), ("btab", btab), ("kfb", kfb)):
                P.dma("sp", tl[:], cin[nm][:], writes=[nm])
            Sb_all = sbt(es, U("Sb_all"), [128, NT, 4, 256], BF16)
            Srun = sbt(es, U("Srun"), [128, 4, 256], F32)
            Sbf = sbt(es, U("Sbf"), [128, 4, 256], BF16)
            qk = [sbt(es, U("qk"), [128, 8, 128], BF16) for _ in range(2)]
            vv = [sbt(es, U("vv"), [128, 1024], BF16) for _ in range(2)]
            rg = [sbt(es, U("rg"), [128, 1024], BF16) for _ in range(2)]
            ra = sbt(es, U("ra"), [128, 8, 64], F32)
            rb = sbt(es, U("rb"), [128, 8, 64], F32)
            qkr = sbt(es, U("qkr"), [128, 8, 128], BF16)
            ksc = sbt(es, U("ksc"), [128, 4, 128], BF16)
            qT = sbt(es, U("qT"), [128, 4, 128], BF16)
            kT = sbt(es, U("kT"), [128, 4, 128], BF16)
            qfT = sbt(es, U("qfT"), [128, 4, 128], BF16)
            qbT = sbt(es, U("qbT"), [128, 4, 128], BF16)
            sT = sbt(es, U("sT"), [128, 4, 128], BF16)
            sqs = sbt(es, U("sqs"), [128, 1024], F32)
            yn = sbt(es, U("yn"), [128, 1024], F32)
            sg = sbt(es, U("sg"), [128, 1024], F32)
            yg = sbt(es, U("yg"), [128, 8, 128], BF16)
            ystg = [sbt(es, U("ystg"), [128, 8, 128], BF16) for _ in range(2)]
            st4 = sbt(es, U("st4"), [128, 6, 4], F32)

            def rope(src, nh, n, key_src):
                cs = ropec[:, n, :].unsqueeze(1).broadcast_to([128, nh, 64])
                sn = ropes[:, n, :].unsqueeze(1).broadcast_to([128, nh, 64])
                t1 = src[:, :, 0:64]; t2 = src[:, :, 64:128]
                tt_op("dve", ra[:, 0:nh, :], t1, cs, ALU.mult, [key_src, "ropec"], ["ra"])
                tt_op("dve", rb[:, 0:nh, :], t2, sn, ALU.mult, [key_src, "ropes"], ["rb"])
                tt_op("dve", qkr[:, 0:nh, 0:64], ra[:, 0:nh, :], rb[:, 0:nh, :], ALU.subtract, ["ra", "rb"], ["qkr1"])
                tt_op("dve", ra[:, 0:nh, :], t1, sn, ALU.mult, [key_src, "ropes"], ["ra"])
                tt_op("dve", rb[:, 0:nh, :], t2, cs, ALU.mult, [key_src, "ropec"], ["rb"])
                tt_op("dve", qkr[:, 0:nh, 64:128], ra[:, 0:nh, :], rb[:, 0:nh, :], ALU.add, ["ra", "rb"], ["qkr2"])

            def kv_update(n, ksrc_ap, which, vb, vkey, store_ap):
                tt_op("dve", ksc[:], ksrc_ap, kfb[:, which, :].unsqueeze(2).broadcast_to([128, 4, 128]), ALU.mult, ["qkr1", "qkr2", "kfb"], ["ksc"])
                (pa, pb2), (ka, kb2) = bank2()
                for h in range(4):
                    pp = pa if h < 2 else pb2; kk = ka if h < 2 else kb2
                    P.op("pe", lambda e, pp=pp, h=h: e.matmul(pp[:, (h % 2) * 256:(h % 2 + 1) * 256], lhsT=ksc[:, h, :], rhs=vb[:, h * 256:(h + 1) * 256], start=True, stop=True),
                         reads=["ksc", vkey], writes=[kk])
                if store_ap is not None:
                    P.op("act", lambda e: e.copy(out=store_ap, in_=Srun[:]), reads=["Srun"], writes=[("Sb_all", n)])
                for h in range(4):
                    pp = pa if h < 2 else pb2; kk = ka if h < 2 else kb2
                    P.op("dve", lambda e, pp=pp, h=h: e.scalar_tensor_tensor(out=Srun[:, h, :], in0=Srun[:, h, :], scalar=GAMMA_C[h], in1=pp[:, (h % 2) * 256:(h % 2 + 1) * 256], op0=ALU.mult, op1=ALU.add),
                         reads=[kk, "Srun"], writes=["Srun"])

            P.op("dve", lambda e: e.memset(Srun[:], 0.0), writes=["Srun"])
            for n in reversed(range(NT)):
                i = n % 2
                r0 = n * 128
                P.dma("sp", qk[i][:, 4:8, :], proj[r0:r0 + 128, 512:1024].rearrange("p (h d) -> p h d", h=4), reads=[("proj", n, 1)], writes=[("qk", i)])
                P.dma("sp", vv[i][:], proj[r0:r0 + 128, 1024:2048], reads=[("proj", n, 2), ("proj", n, 3)], writes=[("vv", i)])
                rope(qk[i][:, 4:8, :], 4, n, ("qk", i))
                kv_update(n, qkr[:, 0:4, :], 1, vv[i], ("vv", i), Sb_all[:, n])
                yield
            P.op("dve", lambda e: e.memset(Srun[:], 0.0), reads=[], writes=["Srun"])
            P.op("dve", lambda e: e.memset(Sbf[:], 0.0), writes=["Sbf"])
            for n in range(0 if os.environ.get('RPASS') == '1' else NT):
                i = n % 2
                r0 = n * 128
                P.dma("sp", qk[i][:], proj[r0:r0 + 128, 0:1024].rearrange("p (h d) -> p h d", h=8), reads=[("proj", n, 0), ("proj", n, 1)], writes=[("qk", i)])
                P.dma("sp", vv[i][:], proj[r0:r0 + 128, 1024:2048], reads=[("proj", n, 2), ("proj", n, 3)], writes=[("vv", i)])
                P.dma("sp", rg[i][:], proj[r0:r0 + 128, 2048:3072], reads=[("proj", n, 4), ("proj", n, 5)], writes=[("rg", i)])
                rope(qk[i][:], 8, n, ("qk", i))
                yield
                pt, pk = transposes(lambda b: qkr[:, b, :], 8, None, None, ["qkr1", "qkr2"])
                ptv = pt[:].rearrange("p (a t) -> p a t", a=8)
                P.op("act", lambda e, ptv=ptv: e.copy(out=qT[:], in_=ptv[:, 0:4, :]), reads=[pk], writes=["qT"])
                P.op("act", lambda e, ptv=ptv: e.copy(out=kT[:], in_=ptv[:, 4:8, :]), reads=[pk], writes=["kT"])
                tt_op("dve", qfT[:], ptv[:, 0:4, :], ftab[:], ALU.mult, [pk, "ftab"], ["qfT"])
                tt_op("dve", qbT[:], ptv[:, 0:4, :], btab[:], ALU.mult, [pk, "btab"], ["qbT"])
                yield
                ps, pks = bank()
                for h in range(4):
                    P.op("pe", lambda e, ps=ps, h=h: e.matmul(ps[:, h * 128:(h + 1) * 128], lhsT=kT[:, h, :], rhs=qT[:, h, :], start=True, stop=True), reads=["kT", "qT"], writes=[pks], inc=(h == 3))
                tt_op("dve", sT[:], ps[:].rearrange("p (h t) -> p h t", h=4), rmask[:], ALU.mult, [pks, "rmask"], ["sT"])
                yield
                (oa, ob), (koa, kob) = bank2()
                for h in range(4):
                    pp = oa if h < 2 else ob; kk = koa if h < 2 else kob
                    osl = pp[:, (h % 2) * 256:(h % 2 + 1) * 256]
                    P.op("pe", lambda e, osl=osl, h=h, i=i: e.matmul(osl, lhsT=sT[:, h, :], rhs=vv[i][:, h * 256:(h + 1) * 256], start=True, stop=False), reads=["sT", ("vv", i)], writes=[kk], inc=False)
                    P.op("pe", lambda e, osl=osl, h=h: e.matmul(osl, lhsT=qfT[:, h, :], rhs=Sbf[:, h, :], start=False, stop=False), reads=["qfT", "Sbf"], writes=[kk], inc=False)
                    P.op("pe", lambda e, osl=osl, h=h, n=n: e.matmul(osl, lhsT=qbT[:, h, :], rhs=Sb_all[:, n, h, :], start=False, stop=True), reads=["qbT", ("Sb_all", n)], writes=[kk], inc=True)
                kv_update(n, qkr[:, 4:8, :], 0, vv[i], ("vv", i), None)
                P.op("act", lambda e: e.copy(out=Sbf[:], in_=Srun[:]), reads=["Srun"], writes=["Sbf"])
                for hf, (pp, kk) in enumerate(((oa, koa), (ob, kob))):
                    P.op("dve", lambda e, pp=pp, hf=hf: e.tensor_reduce(out=st4[:, 0, hf * 2:hf * 2 + 2], in_=pp[:].rearrange("p (h e) -> p h e", h=2), axis=AX.X, op=ALU.add), reads=[kk], writes=["st_sum"])
                    P.op("act", lambda e, pp=pp, hf=hf: e.activation(out=sqs[:, hf * 512:(hf + 1) * 512], in_=pp[:], func=AF.Square), reads=[kk], writes=["sqs"])
                P.op("dve", lambda e: e.tensor_reduce(out=st4[:, 1, :], in_=sqs[:].rearrange("p (h e) -> p h e", h=4), axis=AX.X, op=ALU.add), reads=["sqs"], writes=["st_ssq"])
                P.op("dve", lambda e: e.tensor_scalar(out=st4[:, 2, :], in0=st4[:, 0, :], scalar1=1.0 / 256, scalar2=None, op0=ALU.mult), reads=["st_sum"], writes=["st_mean"])
                tt_op("dve", st4[:, 4, :], st4[:, 2, :], st4[:, 2, :], ALU.mult, ["st_mean"], ["st_msq"])
                P.op("dve", lambda e: e.scalar_tensor_tensor(out=st4[:, 3, :], in0=st4[:, 1, :], scalar=1.0 / 256, in1=st4[:, 4, :], op0=ALU.mult, op1=ALU.subtract), reads=["st_ssq", "st_msq"], writes=["st_var"])
                P.op("act", lambda e: e.activation(out=st4[:, 3, :], in_=st4[:, 3, :], func=AF.Sqrt, bias=epsc[:], scale=1.0), reads=["st_var", "epsc"], writes=["st_var"])
                P.op("dve", lambda e: e.reciprocal(out=st4[:, 3, :], in_=st4[:, 3, :]), reads=["st_var"], writes=["st_var"])
                P.op("dve", lambda e: e.scalar_tensor_tensor(out=st4[:, 5, :], in0=st4[:, 2, :], scalar=-1.0, in1=st4[:, 3, :], op0=ALU.mult, op1=ALU.mult), reads=["st_mean", "st_var"], writes=["st_nmr"])
                for h in range(4):
                    pp = oa if h < 2 else ob; kk = koa if h < 2 else kob
                    P.op("act", lambda e, pp=pp, h=h: e.activation(out=yn[:, h * 256:(h + 1) * 256], in_=pp[:, (h % 2) * 256:(h % 2 + 1) * 256], func=AF.Identity, scale=st4[:, 3, h:h + 1], bias=st4[:, 5, h:h + 1]),
                         reads=[kk, "st_var", "st_nmr"], writes=["yn"])
                yield
                P.op("act", lambda e, i=i: e.activation(out=sg[:], in_=rg[i][:], func=AF.Silu), reads=[("rg", i)], writes=["sg"])
                tt_op("dve", yg[:].rearrange("p a t -> p (a t)"), yn[:], sg[:], ALU.mult, ["yn", "sg"], ["yg"])
                pt2, pk2 = transposes(lambda b: yg[:, b, :], 8, None, None, ["yg"])
                ys = ystg[i]
                P.op("act", lambda e, pt2=pt2, ys=ys: e.copy(out=ys[:].rearrange("p a t -> p (a t)"), in_=pt2[:]), reads=[pk2], writes=[("ystg", i)])
                P.dma("sp", yT[0:8, :, r0:r0 + 128].rearrange("k p t -> p k t"), ys[:], reads=[("ystg", i)], writes=[("yT", 0, n)])
                yield
      yield

    def g_gen(es):
     if True:
      if "G" not in SKIP:
            tri = sbt(es, U("tri"), [128, 4, 128], F32)
            P.dma("sp", tri[:], cin["tri"][:], writes=["tri"])
            wgt = sbt(es, U("wgt"), [32, 512], F32)
            bgt = sbt(es, U("bgt"), [1, 512], F32)
            P.dma("sp", wgt[:], B.wgate[l], writes=["wgt"])
            P.dma("sp", bgt[:], B.bgate[l], writes=["bgt"])
            glr = [sbt(es, U("glr"), [32, 128], F32) for _ in range(2)]
            gqk = [sbt(es, U("gqk"), [128, 512], BF16) for _ in range(2)]
            gv = [sbt(es, U("gv"), [128, 512], BF16) for _ in range(2)]
            gr = [sbt(es, U("gr"), [128, 512], BF16) for _ in range(2)]
            e1 = sbt(es, U("e1"), [128, 512], F32)
            ll = sbt(es, U("ll"), [128, 512], F32)
            Eq = sbt(es, U("Eq"), [128, 512], F32)
            Ek = sbt(es, U("Ek"), [128, 512], F32)
            Est = sbt(es, U("Est"), [128, 256], F32)
            qin = sbt(es, U("qin"), [128, 2, 256], BF16)
            kin = sbt(es, U("kin"), [128, 2, 256], BF16)
            kst = sbt(es, U("kst"), [128, 256], BF16)
            qTm = sbt(es, U("qTm"), [128, 2, 4, 128], BF16)
            kTg = sbt(es, U("kTg"), [128, 4, 128], BF16)
            tm1 = sbt(es, U("tm1"), [128, 4, 128], F32)
            tm2 = sbt(es, U("tm2"), [128, 4, 128], F32)
            sTg = sbt(es, U("sTg"), [128, 4, 128], BF16)
            Sg_all = sbt(es, U("Sg_all"), [128, NT, 2, 128], BF16)
            Sgrun = sbt(es, U("Sgrun"), [128, 2, 128], F32)
            Sgbf = sbt(es, U("Sgbf"), [128, 2, 128], BF16)
            dcol = sbt(es, U("dcol"), [128, 2], F32)
            gsq = sbt(es, U("gsq"), [128, 512], F32)
            gst = sbt(es, U("gst"), [128, 2, 4], F32)
            gyn = sbt(es, U("gyn"), [128, 512], F32)
            gsg = sbt(es, U("gsg"), [128, 512], F32)
            gyg = sbt(es, U("gyg"), [128, 4, 128], BF16)
            gys = [sbt(es, U("gys"), [128, 4, 128], BF16) for _ in range(2)]

            def gates(n, i):
                c0 = n * 128
                P.dma("sp", glr[i][:], glrT[:, c0:c0 + 128], reads=[("glrT", c0 // 512)], writes=[("glr", i)])
                px, pkx = bank()
                P.op("pe", lambda e, px=px, i=i: e.matmul(px[:], lhsT=glr[i][:], rhs=wgt[:], start=True, stop=False), reads=[("glr", i), "wgt"], writes=[pkx], inc=False)
                P.op("pe", lambda e, px=px: e.matmul(px[:], lhsT=onesf[0:1, :], rhs=bgt[:], start=False, stop=True), reads=["onesf", "bgt"], writes=[pkx])
                P.op("act", lambda e, px=px: e.activation(out=e1[:], in_=px[:], func=AF.Exp, scale=-1.0), reads=[pkx], writes=["e1"])
                P.op("act", lambda e: e.activation(out=ll[:], in_=e1[:], func=AF.Ln, bias=1.0), reads=["e1"], writes=["ll"])

            def state_update(n, d, i, store_ap):
                ptot, pkt = bank()
                for pr in range(2):
                    P.op("pe", lambda e, ptot=ptot, pr=pr, d=d: e.matmul(ptot[:, pr:pr + 1], lhsT=ll[:, d * 256 + pr * 128:d * 256 + (pr + 1) * 128], rhs=onesf[:, 0:1], start=True, stop=True),
                         reads=["ll", "onesf"], writes=[pkt], inc=(pr == 1))
                P.op("act", lambda e, ptot=ptot: e.activation(out=dcol[:], in_=ptot[:, 0:2], func=AF.Exp, scale=-1.0 / 16), reads=[pkt], writes=["dcol"])
                pkv, pkk = bank()
                for pr in range(2):
                    P.op("pe", lambda e, pkv=pkv, pr=pr, i=i: e.matmul(pkv[:, pr * 256:(pr + 1) * 256], lhsT=kst[:, pr * 128:(pr + 1) * 128], rhs=gv[i][:, pr * 256:(pr + 1) * 256], start=True, stop=True),
                         reads=["kst", ("gv", i)], writes=[pkk], inc=(pr == 1))
                if store_ap is not None:
                    P.op("act", lambda e: e.copy(out=store_ap, in_=Sgrun[:]), reads=["Sgrun"], writes=[("Sg_all", n)])
                for pr in range(2):
                    for hh in range(2):
                        rs = slice(hh * 64, (hh + 1) * 64)
                        P.op("dve", lambda e, pkv=pkv, pr=pr, hh=hh, rs=rs: e.scalar_tensor_tensor(out=Sgrun[rs, pr, :], in0=Sgrun[rs, pr, :], scalar=dcol[rs, pr:pr + 1], in1=pkv[rs, pr * 256 + hh * 128:pr * 256 + (hh + 1) * 128], op0=ALU.mult, op1=ALU.add),
                             reads=[pkk, "Sgrun", "dcol"], writes=["Sgrun"])

            P.op("dve", lambda e: e.memset(Sgrun[:], 0.0), writes=["Sgrun"])
            for n in reversed(range(NT)):
                i = n % 2; r0 = n * 128
                P.dma("sp", gqk[i][:], proj[r0:r0 + 128, 3584:4096], reads=[("proj", n, 7)], writes=[("gqk", i)])
                P.dma("sp", gv[i][:], proj[r0:r0 + 128, 4096:4608], reads=[("proj", n, 8)], writes=[("gv", i)])
                gates(n, i)
                pr_, pkr = bank()
                P.op("pe", lambda e, pr_=pr_: e.matmul(pr_[:, 0:256], lhsT=tri[:, 3, :], rhs=ll[:, 256:512], start=True, stop=True), reads=["tri", "ll"], writes=[pkr])
                P.op("act", lambda e, pr_=pr_: e.activation(out=Est[:], in_=pr_[:, 0:256], func=AF.Exp, scale=-1.0 / 16), reads=[pkr], writes=["Est"])
                tt_op("dve", kst[:], gqk[i][:, 256:512], Est[:], ALU.mult, [("gqk", i), "Est"], ["kst"])
                state_update(n, 1, i, Sg_all[:, n])
                yield
            P.op("dve", lambda e: e.memset(Sgrun[:], 0.0), writes=["Sgrun"])
            P.op("dve", lambda e: e.memset(Sgbf[:], 0.0), writes=["Sgbf"])
            for n in range(NT):
                i = n % 2; r0 = n * 128
                P.dma("sp", gqk[i][:], proj[r0:r0 + 128, 3584:4096], reads=[("proj", n, 7)], writes=[("gqk", i)])
                P.dma("sp", gv[i][:], proj[r0:r0 + 128, 4096:4608], reads=[("proj", n, 8)], writes=[("gv", i)])
                P.dma("sp", gr[i][:], proj[r0:r0 + 128, 4608:5120], reads=[("proj", n, 9)], writes=[("gr", i)])
                gates(n, i)
                yield
                (pa, pb2), (ka, kb2) = bank2()
                P.op("pe", lambda e, pa=pa: e.matmul(pa[:, 0:256], lhsT=tri[:, 0, :], rhs=ll[:, 0:256], start=True, stop=True), reads=["tri", "ll"], writes=[ka], inc=False)
                P.op("pe", lambda e, pa=pa: e.matmul(pa[:, 256:512], lhsT=tri[:, 2, :], rhs=ll[:, 256:512], start=True, stop=True), reads=["tri", "ll"], writes=[ka])
                P.op("pe", lambda e, pb2=pb2: e.matmul(pb2[:, 0:256], lhsT=tri[:, 1, :], rhs=ll[:, 0:256], start=True, stop=True), reads=["tri", "ll"], writes=[kb2])
                P.op("act", lambda e, pa=pa: e.activation(out=Eq[:], in_=pa[:], func=AF.Exp, scale=-1.0 / 16), reads=[ka], writes=["Eq"])
                P.op("act", lambda e, pa=pa: e.activation(out=Ek[:], in_=pa[:], func=AF.Exp, scale=1.0 / 16), reads=[ka], writes=["Ek"])
                P.op("act", lambda e, pb2=pb2: e.activation(out=Est[:], in_=pb2[:, 0:256], func=AF.Exp, scale=-1.0 / 16), reads=[kb2], writes=["Est"])
                yield
                gq_b = gqk[i][:, 0:256].unsqueeze(1).broadcast_to([128, 2, 256])
                gk_b = gqk[i][:, 256:512].unsqueeze(1).broadcast_to([128, 2, 256])
                P.op("dve", lambda e, gq_b=gq_b: e.scalar_tensor_tensor(out=qin[:], in0=Eq[:].rearrange("p (d c) -> p d c", d=2), scalar=0.125, in1=gq_b, op0=ALU.mult, op1=ALU.mult), reads=["Eq", ("gqk", i)], writes=["qin"])
                tt_op("dve", kin[:], Ek[:].rearrange("p (d c) -> p d c", d=2), gk_b, ALU.mult, ["Ek", ("gqk", i)], ["kin"])
                tt_op("dve", kst[:], gqk[i][:, 256:512], Est[:], ALU.mult, [("gqk", i), "Est"], ["kst"])
                yield
                def blk(b):
                    src = qin if b < 4 else kin
                    bb = b % 4
                    return src[:, bb // 2, (bb % 2) * 128:(bb % 2 + 1) * 128]
                pt, pk = transposes(blk, 8, None, None, ["qin", "kin"])
                ptv = pt[:].rearrange("p (a t) -> p a t", a=8)
                P.op("act", lambda e, ptv=ptv: e.activation(out=qTm[:, 0], in_=ptv[:, 0:4, :], func=AF.Identity, scale=tri[:, 3, 64:65]), reads=[pk, "tri"], writes=["qTm0"])
                P.op("act", lambda e, ptv=ptv: e.activation(out=qTm[:, 1], in_=ptv[:, 0:4, :], func=AF.Identity, scale=tri[:, 2, 64:65]), reads=[pk, "tri"], writes=["qTm1"])
                P.op("dve", lambda e, ptv=ptv: e.tensor_copy(out=kTg[:], in_=ptv[:, 4:8, :]), reads=[pk], writes=["kTg"])
                yield
                (sa, sb2), (ksa, ksb) = bank2()
                for d in range(2):
                    pp = sa if d == 0 else sb2; kk = ksa if d == 0 else ksb
                    for h in range(4):
                        pr = h // 2; hh = h % 2
                        P.op("pe", lambda e, pp=pp, h=h, d=d, pr=pr, hh=hh: e.matmul(pp[:, h * 128:(h + 1) * 128], lhsT=kTg[:, d * 2 + pr, :], rhs=qTm[:, hh, d * 2 + pr, :], start=True, stop=True),
                             reads=["kTg", "qTm0", "qTm1"], writes=[kk], inc=(h == 3))
                tt_op("dve", tm1[:], sa[:].rearrange("p (h t) -> p h t", h=4), tri[:, 0, :].unsqueeze(1).broadcast_to([128, 4, 128]), ALU.mult, [ksa, "tri"], ["tm1"])
                tt_op("dve", tm2[:], sb2[:].rearrange("p (h t) -> p h t", h=4), tri[:, 1, :].unsqueeze(1).broadcast_to([128, 4, 128]), ALU.mult, [ksb, "tri"], ["tm2"])
                tt_op("dve", sTg[:], tm1[:], tm2[:], ALU.add, ["tm1", "tm2"], ["sTg"])
                yield
                po, pko = bank()
                for h in range(4):
                    pr = h // 2; hh = h % 2
                    osl = po[:, h * 128:(h + 1) * 128]
                    P.op("pe", lambda e, osl=osl, h=h, i=i: e.matmul(osl, lhsT=sTg[:, h, :], rhs=gv[i][:, h * 128:(h + 1) * 128], start=True, stop=False), reads=["sTg", ("gv", i)], writes=[pko], inc=False)
                    P.op("pe", lambda e, osl=osl, pr=pr, hh=hh: e.matmul(osl, lhsT=qTm[:, hh, pr, :], rhs=Sgbf[:, pr, :], start=False, stop=False), reads=["qTm0", "qTm1", "Sgbf"], writes=[pko], inc=False)
                    P.op("pe", lambda e, osl=osl, pr=pr, hh=hh, n=n: e.matmul(osl, lhsT=qTm[:, hh, 2 + pr, :], rhs=Sg_all[:, n, pr, :], start=False, stop=True), reads=["qTm0", "qTm1", ("Sg_all", n)], writes=[pko], inc=True)
                state_update(n, 0, i, None)
                P.op("act", lambda e: e.copy(out=Sgbf[:], in_=Sgrun[:]), reads=["Sgrun"], writes=["Sgbf"])
                P.op("act", lambda e, po=po: e.activation(out=gsq[:], in_=po[:], func=AF.Square), reads=[pko], writes=["gsq"])
                P.op("dve", lambda e: e.tensor_reduce(out=gst[:, 0, :], in_=gsq[:].rearrange("p (h e) -> p h e", h=4), axis=AX.X, op=ALU.add), reads=["gsq"], writes=["gst0"])
                P.op("act", lambda e: e.activation(out=gst[:, 1, :], in_=gst[:, 0, :], func=AF.Sqrt, bias=epsc[:], scale=1.0 / 128), reads=["gst0", "epsc"], writes=["gst1"])
                P.op("dve", lambda e: e.reciprocal(out=gst[:, 1, :], in_=gst[:, 1, :]), reads=["gst1"], writes=["gst1"])
                tt_op("dve", gyn[:].rearrange("p (h e) -> p h e", h=4), po[:].rearrange("p (h e) -> p h e", h=4), gst[:, 1, :].unsqueeze(2).broadcast_to([128, 4, 128]), ALU.mult, [pko, "gst1"], ["gyn"])
                P.op("act", lambda e, i=i: e.activation(out=gsg[:], in_=gr[i][:], func=AF.Silu), reads=[("gr", i)], writes=["gsg"])
                tt_op("dve", gyg[:].rearrange("p a t -> p (a t)"), gyn[:], gsg[:], ALU.mult, ["gyn", "gsg"], ["gyg"])
                pt2, pk2 = transposes(lambda b: gyg[:, b, :], 4, None, None, ["gyg"])
                ys = gys[i]
                P.op("act", lambda e, pt2=pt2, ys=ys: e.copy(out=ys[:].rearrange("p a t -> p (a t)"), in_=pt2[:, 0:512]), reads=[pk2], writes=[("gys", i)])
                P.dma("sp", yT[12:16, :, r0:r0 + 128].rearrange("k p t -> p k t"), ys[:], reads=[("gys", i)], writes=[("yT", 2, n)])
                yield
      yield

    with contextlib.ExitStack() as es_rg:
        gens = [r_gen(es_rg), g_gen(es_rg)]
        if os.environ.get("MIX_NOINTER") == "1":
            for g_ in gens:
                for _ in g_:
                    pass
                P.barrier()
        else:
            while gens:
                for g_ in list(gens):
                    try:
                        next(g_)
                    except StopIteration:
                        gens.remove(g_)
        P.barrier()

    if "S5" not in SKIP:
        s5_phase(B, l)
    P.barrier()

    with contextlib.ExitStack() as es:
        wo = sbt(es, U("wo"), [128, KT, D], BF16)
        for q in range(4):
            P.dma("pool", wo[:, :, q * 512:(q + 1) * 512], B.w_out[l, :, q * 512:(q + 1) * 512].rearrange("(k p) n -> p k n", p=128), writes=[("wo", q)])
        ych = [sbt(es, U("ych"), [128, KT, 512], BF16) for _ in range(2)]
        hup = [sbt(es, U("ohup"), [128, 512], F32) for _ in range(3)]
        hi = 0
        for c in range(NC5):
            i = c % 2
            t0 = c * 512
            P.dma("sp", ych[i][:], yT[:, :, t0:t0 + 512].rearrange("k p t -> p k t"), reads=[("yT", a, n) for a in range(3) for n in range(c * 4, c * 4 + 4)], writes=[("ych", i)])
            for f in range(KT):
                hu = hup[hi % 3]; hk = ("ohup", hi % 3); hi += 1
                hkeys = [("hT", f, n) for n in range(c * 4, c * 4 + 4)]
                P.dma("sp", hu[:], hT[f, :, t0:t0 + 512], reads=hkeys, writes=[hk])
                pb, pk = bank()
                for kt in range(KT):
                    P.op("pe", lambda e, pb=pb, kt=kt, f=f, i=i: e.matmul(pb[:], lhsT=wo[:, kt, f * 128:(f + 1) * 128], rhs=ych[i][:, kt, :], start=(kt == 0), stop=(kt == KT - 1)),
                         reads=[("wo", f // 4), ("ych", i)], writes=[pk], inc=(kt == KT - 1))
                P.op("dve", lambda e, pb=pb, hu=hu, f=f: e.scalar_tensor_tensor(out=hu[:], in0=pb[:], scalar=g05[:, 1, f:f + 1], in1=hu[:], op0=ALU.mult, op1=ALU.add), reads=[pk, hk, "g05"], writes=[hk])
                P.dma("sp", hT[f, :, t0:t0 + 512], hu[:], reads=[hk], writes=hkeys)
    P.barrier()


F32 = mybir.dt.float32
BF16 = mybir.dt.bfloat16
AF = mybir.ActivationFunctionType
ALU = mybir.AluOpType
AX = mybir.AxisListType

D = 2048
KT = 16
DFF = 5632
NJ = 44
DIN = 5152
NMOD = 9
EPS = 1e-6
LOG_GAMMA = [math.log1p(-2.0 ** (-5.0 - h)) for h in range(4)]


def host_consts(T):
    c = {}
    NT = T // 128
    pos = np.arange(T, dtype=np.float32)
    inv_freq = (10000.0 ** (-np.arange(0, 128, 2, dtype=np.float32) / 128)).astype(np.float32)
    ang = pos[:, None] * inv_freq[None, :]
    cos = np.cos(ang).astype(np.float32).reshape(NT, 128, 64).transpose(1, 0, 2)
    sin = np.sin(ang).astype(np.float32).reshape(NT, 128, 64).transpose(1, 0, 2)
    c["ropec"] = np.ascontiguousarray(cos)
    c["ropes"] = np.ascontiguousarray(sin)
    idx = np.arange(128, dtype=np.float64)
    lg = np.array(LOG_GAMMA, dtype=np.float64)
    dist = np.abs(idx[:, None] - idx[None, :])
    rmask = np.exp(dist[:, None, :] * lg[None, :, None]) * (128 ** -0.5)
    c["rmask"] = rmask.astype(np.float32)
    ftab = np.exp((idx + 1.0)[None, :] * lg[:, None]) * (128 ** -0.5)
    btab = np.exp((128 - idx)[None, :] * lg[:, None]) * (128 ** -0.5)
    c["ftab"] = np.broadcast_to(ftab[None], (128, 4, 128)).astype(np.float32).copy()
    c["btab"] = np.broadcast_to(btab[None], (128, 4, 128)).astype(np.float32).copy()
    kf = np.exp((127.0 - idx)[:, None] * lg[None, :])
    kb = np.exp(idx[:, None] * lg[None, :])
    c["kfb"] = np.stack([kf, kb], axis=1).astype(np.float32)
    j = idx[:, None]; i = idx[None, :]
    tri = np.stack([(j <= i), (j > i), (j >= i), (j < i)], axis=1).astype(np.float32)
    c["tri"] = tri
    c["iota64"] = np.broadcast_to(np.arange(64, dtype=np.float32)[None], (128, 64)).copy()
    c["ident"] = np.eye(128, dtype=np.float32)
    c["ones"] = np.ones((128, 128), dtype=np.float32)
    return c


CONST_SHAPES = lambda T: {k: v.shape for k, v in host_consts(128 * 1).items()}


class B:
    pass


def build(T, depth, TT=None, HQ=22, dbg=False):
    NT = T // 128
    if TT is None:
        TT = min(T, 1024)
    NTT = T // TT
    NCH = TT // 512
    nc = bass.Bass("TRN2", target_bir_lowering=False)
    P = Prog(nc)
    dr = {}

    def din(name, shape, dt=F32):
        dr[name] = nc.dram_tensor(name, list(shape), dt, kind="ExternalInput").ap()
        return dr[name]

    def dscr(name, shape, dt=F32):
        dr[name] = nc.dram_tensor(name, list(shape), dt, kind="Internal").ap()
        return dr[name]

    L = depth
    x = din("x", [T, D])
    cT = din("cT", [128, KT])
    w_ada = din("w_ada", [L, D, NMOD * D])
    b_ada = din("b_ada", [L, NMOD * D])
    gcols = din("gcols", [L, 3, 128, KT])
    gfin = din("gfin", [128, KT])
    fw = {}
    for nm in ("ffn1_w1", "ffn1_w3", "ffn2_w1", "ffn2_w3"):
        fw[nm] = din(nm, [L, D, DFF])
    for nm in ("ffn1_w2", "ffn2_w2"):
        fw[nm] = din(nm, [L, DFF, D])
    w_in = din("w_in", [L, D, DIN])
    w_out = din("w_out", [L, D, D])
    s5p = din("s5p", [L, 2, 128, 3, 16])
    s5B = din("s5B", [L, 2, 2, 16, 128, 128])
    s5C = din("s5C", [L, 2, 2, 16, 128, 128])
    s5d = din("s5d", [L, 128, 4])
    wglu = din("wglu", [L, 512, 512])
    bglu = din("bglu", [L, 128, 4])
    wgate = din("wgate", [L, 32, 512])
    bgate = din("bgate", [L, 1, 512])
    cshape = {k: v.shape for k, v in host_consts(T).items()}
    cin = {k: din("c_" + k, list(s)) for k, s in cshape.items()}
    out = nc.dram_tensor("out", [T, D], F32, kind="ExternalOutput").ap()
    dr["out"] = out

    hT = dscr("hT", [KT, 128, T])
    modrow = dscr("modrow", [L, NMOD * D])
    proj = dscr("proj", [T, 5120], BF16)
    suT = dscr("suT", [4, 128, T], BF16)
    glrT = dscr("glrT", [32, T])
    yT = dscr("yT", [KT, 128, T], BF16)
    taps = {}
    if dbg:
        taps["yT"] = nc.dram_tensor("tap_yT", [KT, 128, T], BF16, kind="ExternalOutput").ap()
        taps["hT"] = nc.dram_tensor("tap_hT", [KT, 128, T], F32, kind="ExternalOutput").ap()

    uid = [0]

    def U(s):
        uid[0] += 1
        return "%s_%d" % (s, uid[0])

    with contextlib.ExitStack() as top:
        pbf = [top.enter_context(nc.psum_tensor("pb%d" % i, [128, 512], F32)) for i in range(6)]
        pbb = [top.enter_context(nc.psum_tensor("pt%d" % i, [128, 1024], BF16)) for i in range(2)]
        rr = {"f": 0, "b": 0}

        def bank():
            i = rr["f"]; rr["f"] = (i + 1) % 6
            return pbf[i], ("pb", i)

        def bank2():
            i = rr["f"]
            if i % 2:
                i = (i + 1) % 6
            rr["f"] = (i + 2) % 6
            return (pbf[i], pbf[i + 1]), (("pb", i), ("pb", i + 1))

        def bankb():
            i = rr["b"]; rr["b"] = (i + 1) % 2
            return pbb[i], ("pt", i)

        qrr = [0]

        def dq():
            qrr[0] ^= 1
            return "sp" if qrr[0] else "act"

        sbt = lambda es, name, shape, dt: es.enter_context(nc.sbuf_tensor(name, list(shape), dt))
        ident = sbt(top, "ident", [128, 128], BF16)
        onesb = sbt(top, "onesb", [128, 128], BF16)
        onesf = sbt(top, "onesf", [128, 128], F32)
        modc = sbt(top, "modc", [128, NMOD * KT], F32)
        gcol = sbt(top, "gcol", [128, 3, KT], F32)
        acol = sbt(top, "acol", [128, 3, KT], F32)
        g05 = sbt(top, "g05", [128, 3, KT], F32)
        condT = sbt(top, "condT", [128, KT], BF16)
        epsc = sbt(top, "epsc", [128, 1], F32)
        pic = sbt(top, "pic", [128, 1], F32)

        with contextlib.ExitStack() as es:
            tmpf = sbt(es, "tmpf", [128, 128], F32)
            P.dma("sp", tmpf[:], cin["ident"][:], writes=["tmpf"])
            P.op("dve", lambda e: e.tensor_copy(out=ident[:], in_=tmpf[:]), reads=["tmpf"], writes=["ident"])
            P.dma("sp", onesf[:], cin["ones"][:], writes=["onesf"])
            P.op("dve", lambda e: e.tensor_copy(out=onesb[:], in_=onesf[:]), reads=["onesf"], writes=["onesb"])
            P.op("dve", lambda e: e.memset(epsc[:], EPS), writes=["epsc"])
            P.op("dve", lambda e: e.memset(pic[:], math.pi), writes=["pic"])
            cTf = sbt(es, "cTf", [128, KT], F32)
            P.dma("sp", cTf[:], cT[:], writes=["cTf"])
            P.op("act", lambda e: e.activation(out=condT[:], in_=cTf[:], func=AF.Silu), reads=["cTf"], writes=["condT"])

            identf = sbt(es, "identf", [128, 128], F32)
            P.dma("sp", identf[:], cin["ident"][:], writes=["identf"])
            xin = [sbt(es, "xin%d" % i, [128, D], F32) for i in range(2)]
            hst = [sbt(es, "hst%d" % i, [128, KT, 128], F32) for i in range(2)]
            for n in range(NT):
                xb_ = xin[n % 2]; hb_ = hst[n % 2]
                P.dma("sp", xb_[:], x[n * 128:(n + 1) * 128, :], writes=[("xin", n % 2)])
                for g4 in range(4):
                    pb, pk = bank()
                    for q in range(4):
                        kt = g4 * 4 + q
                        P.op("pe", lambda e, pb=pb, q=q, kt=kt, xb_=xb_: e.transpose(out=pb[:, q * 128:(q + 1) * 128], in_=xb_[:, kt * 128:(kt + 1) * 128], identity=identf[:]),
                             reads=[("xin", n % 2), "identf"], writes=[pk], inc=(q == 3))
                    eng = "act" if g4 % 2 else "dve"
                    if eng == "act":
                        P.op("act", lambda e, pb=pb, hb_=hb_, g4=g4: e.copy(out=hb_[:, g4 * 4:(g4 + 1) * 4, :], in_=pb[:].rearrange("p (a b) -> p a b", a=4)), reads=[pk], writes=[("hst", n % 2, g4)])
                    else:
                        P.op("dve", lambda e, pb=pb, hb_=hb_, g4=g4: e.tensor_copy(out=hb_[:, g4 * 4:(g4 + 1) * 4, :], in_=pb[:].rearrange("p (a b) -> p a b", a=4)), reads=[pk], writes=[("hst", n % 2, g4)])
                P.dma("sp", hT[:, :, n * 128:(n + 1) * 128].rearrange("k p t -> p k t"), hb_[:], reads=[("hst", n % 2, g) for g in range(4)], writes=[("hT", k, n) for k in range(KT)])

            wab = [sbt(es, "wab%d" % i, [128, KT, 512], BF16) for i in range(2)]
            brow = [sbt(es, "brow%d" % i, [1, 512], F32) for i in range(2)]
            mrow = [sbt(es, "mrow%d" % i, [1, 512], F32) for i in range(2)]
            it = 0
            for l in range(L):
                for ch in range(NMOD * D // 512):
                    wb_ = wab[it % 2]; br_ = brow[it % 2]; mr_ = mrow[it % 2]
                    P.dma("pool", wb_[:], w_ada[l, :, ch * 512:(ch + 1) * 512].rearrange("(k p) n -> p k n", p=128), writes=[("wab", it % 2)])
                    P.dma("sp", br_[:], b_ada[l:l + 1, ch * 512:(ch + 1) * 512], writes=[("brow", it % 2)])
                    pb, pk = bank()
                    for kt in range(KT):
                        P.op("pe", lambda e, pb=pb, kt=kt, wb_=wb_: e.matmul(pb[0:1, :], lhsT=condT[:, kt:kt + 1], rhs=wb_[:, kt, :], start=(kt == 0), stop=(kt == KT - 1)),
                             reads=[("wab", it % 2), "condT"], writes=[pk], inc=(kt == KT - 1))
                    P.op("dve", lambda e, pb=pb, br_=br_, mr_=mr_: e.tensor_tensor(out=mr_[:], in0=pb[0:1, :], in1=br_[:], op=ALU.add), reads=[pk, ("brow", it % 2)], writes=[("mrow", it % 2)])
                    P.dma("sp", modrow[l:l + 1, ch * 512:(ch + 1) * 512], mr_[:], reads=[("mrow", it % 2)], writes=[("modrow", l)])
                    it += 1
        P.barrier()

        def load_layer_cols(l):
            P.dma("sp", modc[:], modrow[l, :].rearrange("(v p) -> p v", p=128), reads=[("modrow", l)], writes=["modc"], allow_slow_non_contiguous=True)
            P.dma("sp", gcol[:], gcols[l].rearrange("s p k -> p s k"), writes=["gcol"])
            for s in range(3):
                P.op("dve", lambda e, s=s: e.scalar_tensor_tensor(out=acol[:, s, :], in0=modc[:, (3 * s + 1) * KT:(3 * s + 2) * KT], scalar=1.0, in1=gcol[:, s, :], op0=ALU.add, op1=ALU.mult),
                     reads=["modc", "gcol"], writes=["acol"])
                P.op("dve", lambda e, s=s: e.tensor_scalar(out=g05[:, s, :], in0=modc[:, (3 * s + 2) * KT:(3 * s + 3) * KT], scalar1=(1.0 if s == 1 else 0.5), scalar2=None, op0=ALU.mult),
                     reads=["modc"], writes=["g05"])

        def norm_mod(es_bufs, s, t0, ntok, uT, ukey, a_ap=None, b_ap=None):
            hch, sq, rstd = es_bufs
            for c in range(ntok // 512):
                tt0 = t0 + c * 512
                hk = U("hch")
                P.dma("sp", hch[:], hT[:, :, tt0:tt0 + 512].rearrange("k p t -> p k t"), reads=[("hT", k, n) for k in range(KT) for n in range(tt0 // 128, tt0 // 128 + 4)], writes=["hch"])
                P.op("act", lambda e: e.activation(out=sq[:], in_=hch[:], func=AF.Square), reads=["hch"], writes=["sq"])
                pb, pk = bank()
                for kt in range(KT):
                    P.op("pe", lambda e, pb=pb, kt=kt: e.matmul(pb[:], lhsT=onesb[:], rhs=sq[:, kt, :], start=(kt == 0), stop=(kt == KT - 1)), reads=["sq", "onesb"], writes=[pk], inc=(kt == KT - 1))
                P.op("act", lambda e, pb=pb: e.activation(out=rstd[:], in_=pb[:], func=AF.Sqrt, bias=epsc[:], scale=1.0 / D), reads=[pk, "epsc"], writes=["rstd"])
                P.op("dve", lambda e: e.reciprocal(out=rstd[:], in_=rstd[:]), reads=["rstd"], writes=["rstd"])
                P.op("dve", lambda e: e.tensor_tensor(out=hch[:], in0=hch[:], in1=rstd[:].unsqueeze(1).broadcast_to([128, KT, 512]), op=ALU.mult), reads=["hch", "rstd"], writes=["hch"])
                for kt in range(KT):
                    if a_ap is None:
                        sc_ap = acol[:, s, kt:kt + 1]
                        bi_ap = modc[:, (3 * s) * KT + kt:(3 * s) * KT + kt + 1]
                        rd = ["acol", "modc"]
                    else:
                        sc_ap = a_ap[:, kt:kt + 1]; bi_ap = 0.0; rd = ["gfinc"]
                    P.op("act", lambda e, kt=kt, c=c, sc_ap=sc_ap, bi_ap=bi_ap: e.activation(out=uT[:, kt, c * 512:(c + 1) * 512], in_=hch[:, kt, :], func=AF.Identity, scale=sc_ap, bias=bi_ap),
                         reads=["hch"] + rd, writes=[ukey])

        def ffn(l, s, w1, w3, w2):
            with contextlib.ExitStack() as es:
                uTs = [sbt(es, U("uT"), [128, KT, TT], BF16) for _ in range(2)]
                gT = sbt(es, U("gT"), [128, HQ, TT], BF16)
                hch = sbt(es, U("hch"), [128, KT, 512], F32)
                sq = sbt(es, U("sq"), [128, KT, 512], BF16)
                rstd = sbt(es, U("rstd"), [128, 512], F32)
                w13 = [sbt(es, U("w13"), [128, 2, KT, 128], BF16) for _ in range(3)]
                w2b = [sbt(es, U("w2b"), [128, HQ, 128], BF16) for _ in range(2)]
                sil = [sbt(es, U("sil"), [128, 512], F32) for _ in range(2)]
                hup = [sbt(es, U("hup"), [128, 512], F32) for _ in range(3)]
                wi = 0; w2i = 0; si = 0; hi = 0
                norm_mod((hch, sq, rstd), s, 0, TT, uTs[0], ("uT", 0))
                for tt in range(NTT):
                    t0 = tt * TT
                    uT = uTs[tt % 2]; ukey = ("uT", tt % 2)
                    for hp in range(NJ // HQ):
                        for jj in range(HQ):
                            j = hp * HQ + jj
                            wb_ = w13[wi % 3]; wk = ("w13", wi % 3); wi += 1
                            P.dma("pool", wb_[:, 0], w1[l, :, j * 128:(j + 1) * 128].rearrange("(k p) n -> p k n", p=128), writes=[wk])
                            P.dma("pool", wb_[:, 1], w3[l, :, j * 128:(j + 1) * 128].rearrange("(k p) n -> p k n", p=128), writes=[wk])
                            for c in range(NCH):
                                pa, pka = bank(); pb_, pkb = bank()
                                for kt in range(KT):
                                    P.op("pe", lambda e, pa=pa, kt=kt, c=c, wb_=wb_, uT=uT: e.matmul(pa[:], lhsT=wb_[:, 0, kt, :], rhs=uT[:, kt, c * 512:(c + 1) * 512], start=(kt == 0), stop=(kt == KT - 1)),
                                         reads=[wk, ukey], writes=[pka], inc=(kt == KT - 1))
                                for kt in range(KT):
                                    P.op("pe", lambda e, pb_=pb_, kt=kt, c=c, wb_=wb_, uT=uT: e.matmul(pb_[:], lhsT=wb_[:, 1, kt, :], rhs=uT[:, kt, c * 512:(c + 1) * 512], start=(kt == 0), stop=(kt == KT - 1)),
                                         reads=[wk, ukey], writes=[pkb], inc=(kt == KT - 1))
                                sl = sil[si % 2]; sk = ("sil", si % 2); si += 1
                                P.op("act", lambda e, pa=pa, sl=sl: e.activation(out=sl[:], in_=pa[:], func=AF.Silu), reads=[pka], writes=[sk])
                                P.op("dve", lambda e, pb_=pb_, sl=sl, jj=jj, c=c: e.tensor_tensor(out=gT[:, jj, c * 512:(c + 1) * 512], in0=pb_[:], in1=sl[:], op=ALU.mult), reads=[pkb, sk], writes=[("gT", jj)])
                        if hp == 0 and tt + 1 < NTT:
                            norm_mod((hch, sq, rstd), s, (tt + 1) * TT, TT, uTs[(tt + 1) % 2], ("uT", (tt + 1) % 2))
                        for f in range(KT):
                            w2_ = w2b[w2i % 2]; w2k = ("w2b", w2i % 2); w2i += 1
                            P.dma("pool", w2_[:], w2[l, hp * HQ * 128:(hp + 1) * HQ * 128, f * 128:(f + 1) * 128].rearrange("(j p) n -> p j n", p=128), writes=[w2k])
                            for c in range(NCH):
                                tt0 = t0 + c * 512
                                hu = hup[hi % 3]; hk = ("hup", hi % 3); hi += 1
                                hkeys = [("hT", f, n) for n in range(tt0 // 128, tt0 // 128 + 4)]
                                P.dma("sp", hu[:], hT[f, :, tt0:tt0 + 512], reads=hkeys, writes=[hk])
                                pb, pk = bank()
                                for jj in range(HQ):
                                    P.op("pe", lambda e, pb=pb, jj=jj, c=c, w2_=w2_: e.matmul(pb[:], lhsT=w2_[:, jj, :], rhs=gT[:, jj, c * 512:(c + 1) * 512], start=(jj == 0), stop=(jj == HQ - 1)),
                                         reads=[w2k, ("gT", jj)], writes=[pk], inc=(jj == HQ - 1))
                                P.op("dve", lambda e, pb=pb, hu=hu, f=f: e.scalar_tensor_tensor(out=hu[:], in0=pb[:], scalar=g05[:, s, f:f + 1], in1=hu[:], op0=ALU.mult, op1=ALU.add),
                                     reads=[pk, hk, "g05"], writes=[hk])
                                P.dma("sp", hT[f, :, tt0:tt0 + 512], hu[:], reads=[hk], writes=hkeys)
            P.barrier()

        B_ = B()
        B_.__dict__.update(locals())
        for l in range(L):
            load_layer_cols(l)
            ffn(l, 0, fw["ffn1_w1"], fw["ffn1_w3"], fw["ffn1_w2"])
            mixer(B_, l)
            ffn(l, 2, fw["ffn2_w1"], fw["ffn2_w3"], fw["ffn2_w2"])

        with contextlib.ExitStack() as es:
            gfc = sbt(es, "gfc", [128, KT], F32)
            P.dma("sp", gfc[:], gfin[:], writes=["gfinc"])
            identf2 = sbt(es, "identf2", [128, 128], F32)
            P.dma("sp", identf2[:], cin["ident"][:], writes=["identf2"])
            uF = sbt(es, "uF", [128, KT, 512], F32)
            hch = sbt(es, "hchF", [128, KT, 512], F32)
            sq = sbt(es, "sqF", [128, KT, 512], BF16)
            rstd = sbt(es, "rstdF", [128, 512], F32)
            ost = [sbt(es, "ost%d" % i, [128, D], F32) for i in range(2)]
            oi = 0
            for c in range(T // 512):
                norm_mod((hch, sq, rstd), 0, c * 512, 512, uF, "uF", a_ap=gfc)
                for q in range(4):
                    ob = ost[oi % 2]; ok = ("ost", oi % 2); oi += 1
                    for g4 in range(4):
                        pb, pk = bank()
                        for r in range(4):
                            kt = g4 * 4 + r
                            P.op("pe", lambda e, pb=pb, r=r, kt=kt, q=q: e.transpose(out=pb[:, r * 128:(r + 1) * 128], in_=uF[:, kt, q * 128:(q + 1) * 128], identity=identf2[:]),
                                 reads=["uF", "identf2"], writes=[pk], inc=(r == 3))
                        if g4 % 2:
                            P.op("act", lambda e, pb=pb, ob=ob, g4=g4: e.copy(out=ob[:, g4 * 512:(g4 + 1) * 512], in_=pb[:]), reads=[pk], writes=[ok + (g4,)])
                        else:
                            P.op("dve", lambda e, pb=pb, ob=ob, g4=g4: e.tensor_copy(out=ob[:, g4 * 512:(g4 + 1) * 512], in_=pb[:]), reads=[pk], writes=[ok + (g4,)])
                    r0 = c * 512 + q * 128
                    P.dma("sp", out[r0:r0 + 128, :], ob[:], reads=[ok + (g,) for g in range(4)])
        if dbg:
            P.barrier()
            P.dma("sp", taps["yT"][:], yT[:], reads=[("yT", n) for n in range(NT)])
            P.dma("sp", taps["hT"][:], hT[:], reads=[("hT", k, n) for k in range(KT) for n in range(NT)])
        P.emit()
    return nc


def col(v):
    v = np.asarray(v, dtype=np.float32)
    return np.ascontiguousarray(v.reshape(-1, 128).T)


def prep_shared(inp, L, T):
    f32 = np.float32
    m = {}
    for k in ("w_ada", "b_ada", "ffn1_w1", "ffn1_w3", "ffn1_w2", "ffn2_w1", "ffn2_w3", "ffn2_w2", "w_in", "w_out"):
        m[k] = np.ascontiguousarray(np.asarray(inp[k], dtype=f32)[:L])
    m["gcols"] = np.stack([np.stack([col(inp[k][l]) for k in ("g_ffn1", "g_mix", "g_ffn2")]) for l in range(L)])
    m["gfin"] = col(inp["g_final"])
    lam_re = np.asarray(inp["s5_lam_re"], f32)[:L]
    lam_im = np.asarray(inp["s5_lam_im"], f32)[:L]
    ldt = np.asarray(inp["s5_log_dt"], f32)[:L]
    def stl(a):
        return a.reshape(L, 2, 16, 128).transpose(0, 1, 3, 2)
    ldtb = np.broadcast_to(ldt[..., None], (L, 2, 32, 64))
    m["s5p"] = np.ascontiguousarray(np.stack([stl(lam_re), stl(lam_im), stl(ldtb)], axis=3)).astype(f32)
    b_re = np.asarray(inp["s5_b_re"], f32)[:L]; b_im = np.asarray(inp["s5_b_im"], f32)[:L]
    c_re = np.asarray(inp["s5_c_re"], f32)[:L]; c_im = np.asarray(inp["s5_c_im"], f32)[:L]
    Bm = np.zeros((L, 2, 2, 16, 128, 128), f32)
    Cm = np.zeros((L, 2, 2, 16, 128, 128), f32)
    for g in range(32):
        st = g // 2; so = (g % 2) * 64
        co = (g % 8) * 16
        for ri, (bb, cc) in enumerate(((b_re, c_re), (b_im, c_im))):
            Bm[:, :, ri, st, co:co + 16, so:so + 64] = bb[:, :, g].transpose(0, 1, 3, 2)
            Cm[:, :, ri, st, so:so + 64, co:co + 16] = cc[:, :, g].transpose(0, 1, 3, 2)
    m["s5B"] = Bm; m["s5C"] = Cm
    m["s5d"] = np.stack([col(inp["s5_d"][l]) for l in range(L)])
    m["wglu"] = np.ascontiguousarray(np.asarray(inp["s5_w_glu"], f32)[:L])
    m["bglu"] = np.stack([col(inp["s5_b_glu"][l]) for l in range(L)])
    wg = np.asarray(inp["gla_w_gate"], f32)[:L]
    wgb = np.zeros((L, 32, 512), f32)
    wgb[:, 0:16, 0:256] = wg[:, 0]
    wgb[:, 16:32, 256:512] = wg[:, 1]
    m["wgate"] = wgb
    m["bgate"] = np.ascontiguousarray(np.asarray(inp["gla_b_gate"], f32)[:L].reshape(L, 1, 512))
    for k, v in host_consts(T).items():
        m["c_" + k] = v
    return m


def prep_core(inp, shared, b):
    m = dict(shared)
    m["x"] = np.ascontiguousarray(np.asarray(inp["x"][b], dtype=np.float32))
    m["cT"] = col(inp["c"][b])
    return m


SEQ = 4096
DEPTH = 4
N_CORES = 8
_NC_CACHE = {}


def kernel(**inputs):
    inp = {k: np.asarray(v) for k, v in inputs.items()}
    nb = inp["x"].shape[0]
    shared = prep_shared(inp, DEPTH, SEQ)
    hot = [0, 2, 4, 6][:nb]
    real = {c: prep_core(inp, shared, i) for i, c in enumerate(hot)}
    zmap = None
    in_maps = []
    for c in range(N_CORES):
        if c in real:
            in_maps.append(real[c])
        else:
            if zmap is None:
                zmap = {k: np.zeros_like(v) for k, v in real[hot[0]].items()}
            in_maps.append(zmap)
    if "nc" not in _NC_CACHE:
        _NC_CACHE["nc"] = build(SEQ, DEPTH)
    res = run_bass_kernel_spmd(_NC_CACHE["nc"], in_maps, core_ids=list(range(N_CORES)))
    out = np.stack([np.asarray(res.results[c]["out"]) for c in hot], axis=0)
    return out.astype(np.float32)
```
